# Optimizing a Trainium2 kernel written in Bass

```python
import math
import jax, jax.numpy as jnp
from jax import lax
import numpy as np

D_MODEL = 1024
BATCH = 8
SEQ = 2048
DEPTH = 1
DEC_BATCH = 128
DEC_SEQ = 8
PAST_LEN = 16384
PAGE_SIZE = 128

D_INNER = 2 * D_MODEL
HEAD_DIM = 64
N_HEADS = D_INNER // HEAD_DIM
N_GROUPS = 8
HEADS_PER_GROUP = N_HEADS // N_GROUPS
D_STATE = 128
CONV_W = 4
CONV_DIM = D_INNER + 2 * N_GROUPS * D_STATE
CHUNK = 128
POOL_DIM = D_MODEL
POOL_WINDOWS = (2, 4, 8, 16)
N_POOL_GROUPS = len(POOL_WINDOWS)
POOL_GC = POOL_DIM // N_POOL_GROUPS
POOL_BUF = max(POOL_WINDOWS) - 1
N_BRANCH = 2
IN_DIM = D_INNER + CONV_DIM + N_HEADS + POOL_DIM + N_BRANCH * D_MODEL
D_FF = 2816
PLE_DIM = 256
EPS = 1e-6

kernel_name = 'hybrid_ssd_pool_macaron_step'


def _rmsnorm(x, g):
    xf = x.astype(jnp.float32)
    y = xf * lax.rsqrt(jnp.mean(xf * xf, axis=-1, keepdims=True) + EPS)
    return (y * g.astype(jnp.float32)).astype(x.dtype)


def _swiglu(u, w_gu, w_down):
    gate, up = jnp.split(u @ w_gu, 2, axis=-1)
    return (jax.nn.silu(gate) * up) @ w_down


def _ssd(x, dt, A, Bm, Cm, h0):
    b, l = x.shape[:2]
    q = min(CHUNK, l)
    nc = -(-l // q)
    pad = nc * q - l
    if pad:
        padw = lambda a: jnp.pad(a, [(0, 0), (0, pad)] + [(0, 0)] * (a.ndim - 2))
        x, dt, Bm, Cm = padw(x), padw(dt), padw(Bm), padw(Cm)
    rs = lambda a: a.reshape((b, nc, q) + a.shape[2:])
    x, dt, Bm, Cm = rs(x), rs(dt), rs(Bm), rs(Cm)
    a_cs = jnp.cumsum(dt * A, axis=2)
    causal = jnp.tril(jnp.ones((q, q), dtype=bool))
    diff = a_cs[:, :, :, None] - a_cs[:, :, None, :]
    decay = jnp.exp(jnp.where(causal[:, :, None, None], diff, -jnp.inf))
    cb = jnp.einsum('bcqgn,bcsgn->bcqsg', Cm, Bm).astype(jnp.float32)
    w_in = cb[..., None] * decay * dt[:, :, None]
    y_diag = jnp.einsum('bcqsgr,bcsgrp->bcqgrp', w_in, x)
    decay_s = jnp.exp(a_cs[:, :, -1:] - a_cs)
    xw = (decay_s * dt)[..., None] * x
    st = jnp.einsum('bcsgn,bcsgrp->bcgrpn', Bm, xw).astype(jnp.float32)
    chunk_decay = jnp.exp(a_cs[:, :, -1])

    def step(h, inp):
        d, s = inp
        return d[..., None, None] * h + s, h

    h_last, h_prev = lax.scan(step, h0.astype(jnp.float32),
                              (jnp.moveaxis(chunk_decay, 1, 0), jnp.moveaxis(st, 1, 0)))
    y_off = jnp.einsum('bcqgn,cbgrpn->bcqgrp', Cm, h_prev) * jnp.exp(a_cs)[..., None]
    y = (y_diag + y_off).reshape((b, nc * q) + x.shape[3:])[:, :l]
    return y, h_last


def _pool(v, prev, pos0):
    l = v.shape[1]
    ext = jnp.concatenate([prev.astype(v.dtype), v], axis=1)
    cs = jnp.cumsum(ext.astype(jnp.float32), axis=1)
    cs = jnp.pad(cs, [(0, 0), (1, 0), (0, 0)])
    t = jnp.arange(l)
    outs = []
    for gi, w in enumerate(POOL_WINDOWS):
        sl = slice(gi * POOL_GC, (gi + 1) * POOL_GC)
        s = cs[:, POOL_BUF + 1:POOL_BUF + 1 + l, sl] - cs[:, POOL_BUF + 1 - w:POOL_BUF + 1 - w + l, sl]
        cnt = jnp.minimum(pos0 + t + 1, w).astype(jnp.float32)
        outs.append(s / cnt[None, :, None])
    mean = jnp.concatenate(outs, axis=-1)
    return (mean - v.astype(jnp.float32)).astype(v.dtype), ext[:, -POOL_BUF:]


def _layer(x, p, ssm0, conv0, pool0, pos0,
           norm_ffn1, w_ffn1_gu, w_ffn1_down, norm_mix, w_in, conv_w, conv_b,
           dt_bias, a_log, d_skip, norm_ssd, w_ssd_out, w_pool_group, pool_scale,
           w_pool_out, w_o, norm_ffn2, w_ffn2_gu, w_ffn2_down, norm_ple, w_ple_gate, w_ple):
    b, l, _ = x.shape
    h = x + 0.5 * _swiglu(_rmsnorm(x, norm_ffn1), w_ffn1_gu, w_ffn1_down)
    u = _rmsnorm(h, norm_mix)
    proj = u @ w_in
    cuts = np.cumsum([D_INNER, CONV_DIM, N_HEADS, POOL_DIM]).tolist()
    z, xbc, dt_raw, v, gates = jnp.split(proj, cuts, axis=-1)
    ext = jnp.concatenate([conv0.astype(xbc.dtype), xbc], axis=1)
    conv = conv_b + sum(ext[:, k:k + l] * conv_w[k] for k in range(CONV_W))
    new_conv = ext[:, -(CONV_W - 1):]
    xbc = jax.nn.silu(conv)
    xs, Bm, Cm = jnp.split(xbc, [D_INNER, D_INNER + N_GROUPS * D_STATE], axis=-1)
    xs = xs.reshape(b, l, N_GROUPS, HEADS_PER_GROUP, HEAD_DIM)
    Bm = Bm.reshape(b, l, N_GROUPS, D_STATE)
    Cm = Cm.reshape(b, l, N_GROUPS, D_STATE)
    dt = jax.nn.softplus(dt_raw.astype(jnp.float32) + dt_bias.astype(jnp.float32))
    dt = dt.reshape(b, l, N_GROUPS, HEADS_PER_GROUP)
    A = -jnp.exp(a_log.astype(jnp.float32)).reshape(N_GROUPS, HEADS_PER_GROUP)
    h0 = ssm0.reshape(b, N_GROUPS, HEADS_PER_GROUP, HEAD_DIM, D_STATE)
    y, h_last = _ssd(xs, dt, A, Bm, Cm, h0)
    y = y + d_skip.reshape(N_GROUPS, HEADS_PER_GROUP)[..., None] * xs
    y = y.reshape(b, l, D_INNER).astype(x.dtype) * jax.nn.silu(z)
    y = _rmsnorm(y.reshape(b, l, N_GROUPS, D_INNER // N_GROUPS),
                 norm_ssd.reshape(N_GROUPS, D_INNER // N_GROUPS)).reshape(b, l, D_INNER)
    a_branch = y @ w_ssd_out
    pooled, new_pool = _pool(v, pool0, pos0)
    pooled = jnp.einsum('blgc,gcd->blgd', pooled.reshape(b, l, N_POOL_GROUPS, POOL_GC), w_pool_group)
    b_branch = (pooled.reshape(b, l, POOL_DIM) * pool_scale) @ w_pool_out
    g = jax.nn.sigmoid(gates.astype(jnp.float32)).reshape(b, l, N_BRANCH, D_MODEL).astype(x.dtype)
    h = h + (g[:, :, 0] * a_branch + g[:, :, 1] * b_branch) @ w_o
    h = h + 0.5 * _swiglu(_rmsnorm(h, norm_ffn2), w_ffn2_gu, w_ffn2_down)
    pg = jax.nn.sigmoid((_rmsnorm(h, norm_ple) @ w_ple_gate).astype(jnp.float32)).astype(x.dtype)
    h = h + pg * (p @ w_ple)
    new_ssm = h_last.reshape(b, N_HEADS, HEAD_DIM, D_STATE).astype(ssm0.dtype)
    return h, new_ssm, new_conv, new_pool


def setup_inputs(seed: int = 0) -> dict:
    key = jax.random.key(seed)
    ks = iter(jax.random.split(key, 48))
    f32 = jnp.float32
    nrm = lambda shape, s: jax.random.normal(next(ks), shape, f32) * s
    gain = lambda shape: 1.0 + nrm(shape, 0.02)
    d = {}
    d['x_prompt'] = nrm((BATCH, SEQ, D_MODEL), 1.0)
    d['x_sample'] = nrm((DEC_BATCH, DEC_SEQ, D_MODEL), 1.0)
    d['state_ssm'] = nrm((DEPTH, DEC_BATCH, N_HEADS, HEAD_DIM, D_STATE), 0.1)
    d['state_conv'] = nrm((DEPTH, DEC_BATCH, CONV_W - 1, CONV_DIM), 1.0)
    d['state_pool'] = nrm((DEPTH, DEC_BATCH, POOL_BUF, POOL_DIM), 1.0)
    d['p_prompt'] = nrm((DEPTH, BATCH, SEQ, PLE_DIM), 1.0)
    d['p_sample'] = nrm((DEPTH, DEC_BATCH, DEC_SEQ, PLE_DIM), 1.0)
    d['norm_ffn1'] = gain((DEPTH, D_MODEL))
    d['w_ffn1_gu'] = nrm((DEPTH, D_MODEL, 2 * D_FF), D_MODEL ** -0.5)
    d['w_ffn1_down'] = nrm((DEPTH, D_FF, D_MODEL), D_FF ** -0.5)
    d['norm_mix'] = gain((DEPTH, D_MODEL))
    d['w_in'] = nrm((DEPTH, D_MODEL, IN_DIM), D_MODEL ** -0.5)
    d['conv_w'] = nrm((DEPTH, CONV_W, CONV_DIM), CONV_W ** -0.5)
    d['conv_b'] = nrm((DEPTH, CONV_DIM), 0.01)
    dt0 = jnp.exp(jax.random.uniform(next(ks), (DEPTH, N_HEADS), f32,
                                     minval=math.log(1e-3), maxval=math.log(1e-1)))
    d['dt_bias'] = dt0 + jnp.log(-jnp.expm1(-dt0))
    d['a_log'] = jnp.log(jax.random.uniform(next(ks), (DEPTH, N_HEADS), f32, minval=1.0, maxval=16.0))
    d['d_skip'] = gain((DEPTH, N_HEADS))
    d['norm_ssd'] = gain((DEPTH, D_INNER))
    d['w_ssd_out'] = nrm((DEPTH, D_INNER, D_MODEL), D_INNER ** -0.5)
    d['w_pool_group'] = nrm((DEPTH, N_POOL_GROUPS, POOL_GC, POOL_GC), POOL_GC ** -0.5)
    d['pool_scale'] = gain((DEPTH, POOL_DIM))
    d['w_pool_out'] = nrm((DEPTH, POOL_DIM, D_MODEL), POOL_DIM ** -0.5)
    d['w_o'] = nrm((DEPTH, D_MODEL, D_MODEL), D_MODEL ** -0.5)
    d['norm_ffn2'] = gain((DEPTH, D_MODEL))
    d['w_ffn2_gu'] = nrm((DEPTH, D_MODEL, 2 * D_FF), D_MODEL ** -0.5)
    d['w_ffn2_down'] = nrm((DEPTH, D_FF, D_MODEL), D_FF ** -0.5)
    d['norm_ple'] = gain((DEPTH, D_MODEL))
    d['w_ple_gate'] = nrm((DEPTH, D_MODEL, D_MODEL), D_MODEL ** -0.5)
    d['w_ple'] = nrm((DEPTH, PLE_DIM, D_MODEL), PLE_DIM ** -0.5)
    d['norm_final'] = gain((D_MODEL,))
    return d


def reference(x_prompt, x_sample, state_ssm, state_conv, state_pool, p_prompt, p_sample,
              norm_ffn1, w_ffn1_gu, w_ffn1_down, norm_mix, w_in, conv_w, conv_b,
              dt_bias, a_log, d_skip, norm_ssd, w_ssd_out, w_pool_group, pool_scale,
              w_pool_out, w_o, norm_ffn2, w_ffn2_gu, w_ffn2_down, norm_ple, w_ple_gate,
              w_ple, norm_final):
    hp, hs = x_prompt, x_sample
    ssm_p, conv_p, pool_p, ssm_s, conv_s, pool_s = [], [], [], [], [], []
    for i in range(DEPTH):
        w = (norm_ffn1[i], w_ffn1_gu[i], w_ffn1_down[i], norm_mix[i], w_in[i], conv_w[i], conv_b[i],
             dt_bias[i], a_log[i], d_skip[i], norm_ssd[i], w_ssd_out[i], w_pool_group[i],
             pool_scale[i], w_pool_out[i], w_o[i], norm_ffn2[i], w_ffn2_gu[i], w_ffn2_down[i],
             norm_ple[i], w_ple_gate[i], w_ple[i])
        z_ssm = jnp.zeros((BATCH, N_HEADS, HEAD_DIM, D_STATE), state_ssm.dtype)
        z_conv = jnp.zeros((BATCH, CONV_W - 1, CONV_DIM), hp.dtype)
        z_pool = jnp.zeros((BATCH, POOL_BUF, POOL_DIM), hp.dtype)
        hp, s1, c1, q1 = _layer(hp, p_prompt[i], z_ssm, z_conv, z_pool, 0, *w)
        hs, s2, c2, q2 = _layer(hs, p_sample[i], state_ssm[i], state_conv[i], state_pool[i], PAST_LEN, *w)
        ssm_p.append(s1); conv_p.append(c1); pool_p.append(q1)
        ssm_s.append(s2); conv_s.append(c2); pool_s.append(q2)
    y_prompt = _rmsnorm(hp, norm_final)
    y_sample = _rmsnorm(hs, norm_final)
    return (y_prompt, y_sample, jnp.stack(ssm_p), jnp.stack(conv_p), jnp.stack(pool_p),
            jnp.stack(ssm_s), jnp.stack(conv_s), jnp.stack(pool_s))
```

```python
import numpy as np
from contextlib import ExitStack
import concourse.bass as bass
import concourse.mybir as mybir
from concourse.bass_utils import run_bass_kernel_spmd

F32 = mybir.dt.float32
BF16 = mybir.dt.bfloat16
AF = mybir.ActivationFunctionType
ALU = mybir.AluOpType

T = 512
NCH = T // 128
NBLK = 2048 // T
ENGS = ('pe', 'act', 'dve', 'pool', 'sp')
NS_DMA = 24
EPS = 1e-6
DFF = 2816
NF = DFF // 128
PROMPT_BLOCKS = None
DO_SAMPLE = True
STOP_AT = 99
STOP_S = 99
USE_SCRATCH = True


class R:
    __slots__ = ('w', 'rd', 'psum')

    def __init__(self, psum=False):
        self.w = None
        self.rd = []
        self.psum = psum


class Ins:
    __slots__ = ('eng', 'fn', 'pos', 'dma', 'slot', 'val', 'waits', 'signal')


def _flat(res, out):
    for r in res:
        if r is None:
            continue
        if isinstance(r, (tuple, list)):
            _flat(r, out)
        else:
            out.append(r)
    return out


class V:
    def __init__(self, ap, *res):
        self.ap = ap
        self.res = _flat(res, [])


class Prog:
    def __init__(self):
        self.ins = {e: [] for e in ENGS}
        self.waited = {e: {} for e in ENGS}
        self.dma_rr = {e: 0 for e in ENGS}
        self.dma_last = {e: [None] * NS_DMA for e in ENGS}
        self.dma_cnt = {e: [0] * NS_DMA for e in ENGS}
        self.outs = []

    def add(self, eng, fn, reads, writes, dma=False, is_out=False):
        i = Ins()
        i.eng, i.fn, i.dma, i.signal, i.val, i.slot = eng, fn, dma, False, 0, 0
        i.pos = len(self.ins[eng])
        rr, ww = [], []
        for v in reads:
            rr.extend(v.res if isinstance(v, V) else _flat([v], []))
        for v in writes:
            ww.extend(v.res if isinstance(v, V) else _flat([v], []))
        rr = list(dict.fromkeys(rr))
        ww = list(dict.fromkeys(ww))
        deps = []
        for r in rr:
            if r.w is not None:
                deps.append(r.w)
            if r.psum:
                for x in r.rd:
                    if x.eng != eng:
                        deps.append(x)
        same_ok = (eng == 'pe')
        for r in ww:
            if r.w is not None and (r.w.eng != eng or r.w.dma or dma or not same_ok):
                deps.append(r.w)
            for x in r.rd:
                if x.eng != eng or x.dma or dma or not same_ok:
                    deps.append(x)
        if dma:
            j = self.dma_rr[eng] % NS_DMA
            self.dma_rr[eng] += 1
            if self.dma_last[eng][j] is not None:
                deps.append(self.dma_last[eng][j])
            self.dma_cnt[eng][j] += 16
            i.slot, i.val = j, self.dma_cnt[eng][j]
            self.dma_last[eng][j] = i
        waits = {}
        for d in deps:
            if d is i:
                continue
            key = ('d', d.eng, d.slot) if d.dma else ('c', d.eng)
            v = d.val if d.dma else d.pos
            if self.waited[eng].get(key, -1) >= v:
                continue
            if key not in waits or (waits[key].val if d.dma else waits[key].pos) < v:
                waits[key] = d
        for key, d in waits.items():
            self.waited[eng][key] = d.val if d.dma else d.pos
            d.signal = True
        i.waits = waits
        for r in rr:
            r.rd.append(i)
        for r in ww:
            r.w = i
            r.rd = []
        self.ins[eng].append(i)
        if is_out:
            self.outs.append(i)
        return i

    def emit(self, nc, sems, dsems):
        for e in ENGS:
            c = 0
            for i in self.ins[e]:
                if not i.dma and i.signal:
                    c += 1
                    i.val = c
        outs = self.outs

        def run(e, eng):
            for i in self.ins[e]:
                for key, d in i.waits.items():
                    sem = sems[key[1]] if key[0] == 'c' else dsems[key[1]][key[2]]
                    eng.wait_ge(sem, d.val)
                bi = i.fn(eng)
                if i.dma:
                    bi.then_inc(dsems[e][i.slot], 16)
                elif i.signal:
                    bi.then_inc(sems[e], 1)
            if e == 'sp':
                done = {}
                for d in outs:
                    k = (d.eng, d.slot)
                    done[k] = max(done.get(k, 0), d.val)
                for (q, j), v in done.items():
                    eng.wait_ge(dsems[q][j], v)

        with nc.Block() as block:
            @block.tensor
            def _(pe):
                run('pe', pe)

            @block.scalar
            def _(act):
                run('act', act)

            @block.vector
            def _(dve):
                run('dve', dve)

            @block.gpsimd
            def _(pool):
                run('pool', pool)

            @block.sync
            def _(sp):
                run('sp', sp)


class Buf:
    def __init__(self, t, n):
        self.t = t
        self.r = [R() for _ in range(n)]


def _consts():
    c = {}
    c['ident'] = np.eye(128, dtype=np.float32)
    s = np.arange(128)
    c['m01p'] = (s[None, :] >= s[:, None]).astype(np.float32)
    c['m01s'] = ((s[None, :] >= s[:, None]) & (s[None, :] // 8 == s[:, None] // 8)).astype(np.float32)
    cm = np.ones((32, 512), np.float32)
    cm[:, ::128] = 0.0
    c['cmp'] = cm
    cm = np.ones((32, 128), np.float32)
    cm[:, ::8] = 0.0
    c['cms'] = cm
    ic = np.zeros((128, 4, 16), np.float32)
    for gi, w in enumerate((2, 4, 8, 16)):
        ic[:, gi, :] = 1.0 / np.minimum(np.arange(16) + 1, w)
    c['ic16'] = ic.reshape(128, 64)
    c['negp'] = np.tile(np.where(s[None, :] >= s[:, None], 0.0, -60000.0).astype(np.float32), (1, 4))
    c['negs'] = np.tile(np.where((s[None, :] >= s[:, None]) & (s[None, :] // 8 == s[:, None] // 8), 0.0, -60000.0).astype(np.float32), (1, 4))
    c['rowsel'] = (s[:, None] // 8 == np.arange(16)[None, :]).astype(np.float32)
    bd = (s[None, :] // 8 == np.arange(16)[:, None]).astype(np.float32)
    return c


def build(debug=False):
    nc = bass.Bass("TRN2", target_bir_lowering=False)
    P = Prog()
    es = ExitStack()

    def din(name, shape):
        return nc.dram_tensor(name, list(shape), F32, kind="ExternalInput").ap()

    def dout(name, shape):
        return nc.dram_tensor(name, list(shape), F32, kind="ExternalOutput").ap()

    xp = din("xp", [2048, 1024]); xs = din("xs", [128, 1024])
    ssm0 = din("ssm0", [16, 32, 64, 128]); conv0 = din("conv0", [48, 4096]); pool0 = din("pool0", [16, 15, 1024])
    pp = din("pp", [2048, 256]); psm = din("psm", [128, 256])
    w_gu1 = din("w_gu1", [1024, 2 * DFF]); w_d1 = din("w_d1", [DFF, 1024])
    w_gu2 = din("w_gu2", [1024, 2 * DFF]); w_d2 = din("w_d2", [DFF, 1024])
    w_in = din("w_in", [1024, 9248]); w_so = din("w_so", [2048, 1024])
    w_pg = din("w_pg", [4, 256, 256]); w_po = din("w_po", [1024, 1024]); w_o = din("w_o", [1024, 1024])
    w_plg = din("w_plg", [1024, 1024]); w_ple = din("w_ple", [256, 1024])
    vec = din("vec", [128, 240]); s32 = din("s32", [32, 3])
    cst = {k: din("c_" + k, list(v.shape)) for k, v in _consts().items()}

    yp = dout("yp", [2048, 1024]); ys = dout("ys", [128, 1024])
    o_ssm_p = dout("o_ssm_p", [32, 64, 128]); o_conv_p = dout("o_conv_p", [3, 4096]); o_pool_p = dout("o_pool_p", [15, 1024])
    o_ssm_s = dout("o_ssm_s", [16, 32, 64, 128]); o_conv_s = dout("o_conv_s", [16, 3, 4096]); o_pool_s = dout("o_pool_s", [16, 15, 1024])

    def sb(name, shape, dt=F32, n=1):
        t = es.enter_context(nc.sbuf_tensor("sb_" + name, list(shape), dt))
        return Buf(t, n)

    banks = [Buf(es.enter_context(nc.psum_tensor("bank%d" % i, [128, 512], F32)), 1) for i in range(8)]
    for b_ in banks:
        b_.r[0].psum = True
    bank_rr = [0]

    def psum():
        b = banks[bank_rr[0] % 8]
        bank_rr[0] += 1
        return b

    def mm(out, lhsT, rhs, start=True, stop=True):
        P.add('pe', lambda e: e.matmul(out.ap, lhsT.ap, rhs.ap, start=start, stop=stop), [lhsT, rhs], [out])

    def tr(out, in_, ident):
        P.add('pe', lambda e: e.transpose(out.ap, in_.ap, ident.ap), [in_, ident], [out])

    def act(out, in_, func, bias=None, scale=1.0):
        rd = [in_]
        kw = {}
        if isinstance(bias, V):
            rd.append(bias); kw['bias'] = bias.ap
        elif bias is not None:
            kw['bias'] = bias
        if isinstance(scale, V):
            rd.append(scale); kw['scale'] = scale.ap
        else:
            kw['scale'] = scale
        P.add('act', lambda e: e.activation(out.ap, in_.ap, func, **kw), rd, [out])

    def tt(eng, out, in0, in1, op):
        P.add(eng, lambda e: e.tensor_tensor(out.ap, in0.ap, in1.ap, op), [in0, in1], [out])

    def ts(eng, out, in0, s1, s2, op0, op1=ALU.bypass):
        rd = [in0]
        a1 = s1.ap if isinstance(s1, V) else s1
        a2 = s2.ap if isinstance(s2, V) else s2
        if isinstance(s1, V): rd.append(s1)
        if isinstance(s2, V): rd.append(s2)
        P.add(eng, lambda e: e.tensor_scalar(out.ap, in0.ap, a1, a2, op0, op1), rd, [out])

    def stt(out, in0, sc, in1, op0, op1):
        rd = [in0, in1]
        a = sc.ap if isinstance(sc, V) else sc
        if isinstance(sc, V): rd.append(sc)
        P.add('dve', lambda e: e.scalar_tensor_tensor(out.ap, in0.ap, a, in1.ap, op0, op1), rd, [out])

    def cp(eng, out, in_):
        if eng == 'act':
            P.add('act', lambda e: e.copy(out.ap, in_.ap), [in_], [out])
        else:
            P.add(eng, lambda e: e.tensor_copy(out.ap, in_.ap), [in_], [out])

    def memset(eng, out, c):
        P.add(eng, lambda e: e.memset(out.ap, c), [], [out])

    def dma(q, out, in_, is_out=False, slow=False):
        kw = {'allow_slow_non_contiguous': True} if slow else {}
        P.add(q, lambda e: e.dma_start(out=out.ap, in_=in_.ap, **kw), [in_], [out], dma=True, is_out=is_out)

    ident_f = sb("ident_f", [128, 128]); ident_b = sb("ident_b", [128, 128], BF16); ones_b = sb("ones_b", [128, 128], BF16)
    m01p = sb("m01p", [128, 128]); m01s = sb("m01s", [128, 128])
    cmp_ = sb("cmp", [32, T]); cms = sb("cms", [32, 128]); ic16 = sb("ic16", [128, 64]); rowsel = sb("rowsel", [128, 16])
    vec_t = sb("vec", [128, 240]); s32_t = sb("s32", [32, 3]); A32 = sb("A32", [32, 1])
    wdt = sb("wdt", [128, 8, 32], BF16)
    eps_t = sb("eps", [128, 1]); one_t = sb("one", [128, 1])
    S = sb("S", [128, 16, 128], F32, 16)
    ST16 = sb("ST16", [128, 8, 256], BF16, 8)
    ctail = sb("ctail", [128, 32, 3], F32, 32)
    ptail = sb("ptail", [128, 8, 15], F32, 8)

    def Vb(b, ap=None, i=0):
        return V(b.t[:] if ap is None else ap, b.r[i])

    for nm, b in (("ident", ident_f), ("m01p", m01p), ("m01s", m01s), ("cms", cms), ("ic16", ic16), ("rowsel", rowsel)):
        dma('sp', Vb(b), V(cst[nm]))
    dma('sp', Vb(cmp_), V(cst["cmp"][:, 0:T]))
    dma('sp', Vb(vec_t), V(vec)); dma('sp', Vb(s32_t), V(s32))
    negp = sb("negp", [128, 512], BF16); negs = sb("negs", [128, 512], BF16)
    dma('pool', Vb(negp), V(cst["negp"])); dma('pool', Vb(negs), V(cst["negs"]))
    dma('pool', Vb(ident_b), V(cst["ident"]))
    dma('pool', V(wdt.t[:], wdt.r[0]), V(w_in[:, 6144:6176].rearrange("(kc p) n -> p kc n", p=128)))
    memset('dve', Vb(ones_b), 1.0); memset('dve', Vb(eps_t), EPS); memset('dve', Vb(one_t), 1.0)
    act(Vb(A32), V(s32_t.t[:, 1:2], s32_t.r[0]), AF.Exp)
    ts('dve', Vb(A32), Vb(A32), -1.0, None, ALU.mult)
    selp = sb("selp", [32, 16, 128])
    for j in range(16):
        for hh_ in range(2):
            P.add('dve', (lambda j, hh_: lambda e: e.tensor_copy(selp.t[:, j, hh_ * 64:(hh_ + 1) * 64],
                                                               ident_f.t[0:32, 2 * j + hh_:2 * j + hh_ + 1].to_broadcast([32, 64])))(j, hh_),
                  [ident_f.r[0]], [selp.r[0]])
    VC = dict(g_ffn1=0, g_mix=8, g_ffn2=16, g_ple=24, g_final=32, pscale=40, g_ssd=48, conv_b=64, conv_w=96, dskip=224)

    def vcol(name, c):
        k = VC[name] + c
        return V(vec_t.t[:, k:k + 1], vec_t.r[0])

    NW = 4
    wsl = [sb("wsl%d" % i, [128, 8, 512], BF16) for i in range(NW)]
    w_rr = [0]

    NLOADS = 96
    wscr = nc.dram_tensor("wscr", [NLOADS, 128, 8, 512], BF16).ap()
    wscr_r = [R() for _ in range(NLOADS)]
    cur_blk = [0]
    w_idx = [0]

    def load_w(parts):
        s = wsl[w_rr[0] % NW]
        w_rr[0] += 1
        i = w_idx[0]
        w_idx[0] += 1
        assert i < NLOADS
        nkm = max(nk for (_, nk, _) in parts)
        cm = max(c0 + ap2.shape[1] for (ap2, _, c0) in parts)
        if cur_blk[0] == 0 or not USE_SCRATCH:
            for (ap2, nk, c0) in parts:
                ncol = ap2.shape[1]
                dma('pool', V(s.t[:, 0:nk, c0:c0 + ncol], s.r[0]), V(ap2.rearrange("(kc p) n -> p kc n", p=128)))
            if USE_SCRATCH:
                dma('sp', V(wscr[i, :, 0:nkm, 0:cm], wscr_r[i]), V(s.t[:, 0:nkm, 0:cm], s.r[0]))
        else:
            dma('pool', V(s.t[:, 0:nkm, 0:cm], s.r[0]), V(wscr[i, :, 0:nkm, 0:cm], wscr_r[i]))
        return s

    SEG = 1024
    A_SZ = 23552
    B_SZ = 34816
    arena_t = es.enter_context(nc.sbuf_tensor("sb_arena", [128, (A_SZ + B_SZ) // 2], BF16))
    segs = [R() for _ in range((A_SZ + B_SZ) // SEG)]

    def carve(off, shape, dt):
        esz = 2 if dt == BF16 else 4
        free = 1
        for d_ in shape[1:]:
            free *= d_
        nb = free * esz
        assert off % 4 == 0 and off + nb <= A_SZ + B_SZ, (off, shape)
        ap = arena_t[:, off // 2:(off + nb) // 2]
        if dt != BF16:
            ap = ap.bitcast(dt)
        if len(shape) == 3:
            ap = ap.rearrange("p (a b) -> p a b", a=shape[1])
        elif len(shape) == 4:
            ap = ap.rearrange("p (a b c) -> p a b c", a=shape[1], b=shape[2])
        b = Buf(ap, 0)
        row = nb // shape[1] if len(shape) >= 3 else nb
        nrow = shape[1] if len(shape) >= 3 else 1
        b.r = [tuple(segs[(off + k * row) // SEG:(off + (k + 1) * row - 1) // SEG + 1]) for k in range(nrow)]
        return b
    B0 = A_SZ
    h = sb("h", [128, 8, T], F32, 8)
    xn = sb("xn", [128, 8, T], BF16, 8)
    sq8 = carve(0, [128, 8, T], BF16)
    sq2 = carve(16384, [128, 2, T], BF16)
    rs = [sb("rs%d" % i, [128, T]) for i in range(2)]
    rs_rr = [0]
    hid = carve(0, [128, NF, T], BF16)
    tmpA = [sb("tmpA%d" % i, [128, T]) for i in range(3)]
    tmp_rr = [0]
    xst = [sb("xst%d" % i, [128, 1024]) for i in range(2)]
    xst_rr = [0]
    pst = [sb("pst%d" % i, [128, 256]) for i in range(2)]
    p_fm = carve(B0, [128, 2, T], BF16)
    yfin = carve(0, [128, 8, T], F32)
    gs = carve(B0, [128, 8, T], F32)
    vs = carve(0, [128, 8, 128], F32)
    dtb = {k: sb("dt_" + k, [32, T]) for k in ("dt", "acs", "lb", "ea", "dtds")}
    dtb["dtA"] = dtb["lb"]
    tm32 = sb("tm32", [128, NCH, 3, 32])
    cdpm = sb("cdpm", [128, 16, 16])
    pre16 = carve(B0, [128, 4, 4 + T], BF16)
    pre_s = carve(B0, [128, 4, 176], F32)
    cv_s = carve(B0 + 2816, [128, 4, 128], F32)
    xc_pp = [carve(B0 + 4128 + i * 4096, [128, 4, T], BF16) for i in range(2)]
    xc_s = carve(B0 + 4864, [128, 4, 128], BF16)
    sz_pp = [carve(B0 + 12320 + i * 4096, [128, 2, T], F32) for i in range(2)]
    sz_s = carve(B0 + 5888, [128, 2, 128], F32)
    y2_p = carve(B0 + 20512, [128, 2, T], F32)
    DG = [carve(B0 + 24608 + i * 4096, [128, 16, 128], BF16) for i in range(2)]
    for dg_ in DG:
        dg_.r = [tuple(r_ for t_ in dg_.r for r_ in t_)]
    y2_s = carve(B0 + 6912, [128, 2, 128], F32)
    y_all = carve(0, [128, 16, T], BF16)
    ckb = dict(x_tm=sb("ck_x_tm", [128, NCH, 256], BF16), xw=sb("ck_xw", [128, NCH, 256], BF16),
               B_tm=sb("ck_B_tm", [128, NCH, 128], BF16), CBm=sb("ck_CBm", [128, NCH, 128], F32),
               W=sb("ck_W", [128, 4, NCH, 128], BF16, 4), y1=sb("ck_y1", [128, NCH, 256], F32),
               yo=[sb("ck_yo%d" % i, [128, 256], F32) for i in range(2)])
    pooled = carve(B0 + 25296, [128, 8, T], BF16)
    pooled2 = sb("pooled2", [128, 8, T], BF16, 8)
    merged = carve(B0 + 16384, [128, 8, T], BF16)

    def tmp():
        b = tmpA[tmp_rr[0] % 3]
        tmp_rr[0] += 1
        return b

    def rmsnorm(src, nk, D, gname, gofs, dst, n, sq=None):
        sq = sq if sq is not None else sq8
        for k in range(nk):
            act(V(sq.t[:, k, 0:n], sq.r[k]), V(src.t[:, k, 0:n], src.r[k]), AF.Square)
        ps = psum()
        for k in range(nk):
            mm(V(ps.t[:, 0:n], ps.r[0]), Vb(ones_b), V(sq.t[:, k, 0:n], sq.r[k]), start=(k == 0), stop=(k == nk - 1))
        r = rs[rs_rr[0] % 2]
        rs_rr[0] += 1
        act(V(r.t[:, 0:n], r.r[0]), V(ps.t[:, 0:n], ps.r[0]), AF.Ln, bias=Vb(eps_t), scale=1.0 / D)
        act(V(r.t[:, 0:n], r.r[0]), V(r.t[:, 0:n], r.r[0]), AF.Exp, scale=-0.5)
        for k in range(nk):
            stt(V(dst.t[:, k, 0:n], dst.r[k]), V(src.t[:, k, 0:n], src.r[k]), vcol(gname, gofs + k), V(r.t[:, 0:n], r.r[0]),
                ALU.mult, ALU.mult)

    def linear(w2d, K, c0, ncols, src, n, evac):
        nk_tot = K
        ktiles = [(k0, min(8, nk_tot - k0)) for k0 in range(0, nk_tot, 8)]
        for ct in range(0, ncols, 512):
            nc_ = min(512, ncols - ct)
            nm = nc_ // 128
            pss = [psum() for _ in range(nm)]
            for (k0, nk) in ktiles:
                slot = load_w([(w2d[k0 * 128:(k0 + nk) * 128, c0 + ct:c0 + ct + nc_], nk, 0)])
                for m in range(nm):
                    for kk in range(nk):
                        k = k0 + kk
                        mm(V(pss[m].t[:, 0:n], pss[m].r[0]), V(slot.t[:, kk, m * 128:(m + 1) * 128], slot.r[0]),
                           V(src.t[:, k, 0:n], src.r[k]), start=(k == 0), stop=(k == nk_tot - 1))
            for m in range(nm):
                evac((ct // 128) + m, V(pss[m].t[:, 0:n], pss[m].r[0]))

    def ffn(w_gu, w_d, gname, n):
        rmsnorm(h, 8, 1024, gname, 0, xn, n)
        for jt in range(0, NF, 4):
            nf = min(4, NF - jt)
            wg = load_w([(w_gu[:, jt * 128:(jt + nf) * 128], 8, 0)])
            wu = load_w([(w_gu[:, DFF + jt * 128:DFF + (jt + nf) * 128], 8, 0)])
            for f in range(nf):
                pg = psum(); pu = psum()
                for k in range(8):
                    mm(V(pg.t[:, 0:n], pg.r[0]), V(wg.t[:, k, f * 128:(f + 1) * 128], wg.r[0]), V(xn.t[:, k, 0:n], xn.r[k]),
                       start=(k == 0), stop=(k == 7))
                for k in range(8):
                    mm(V(pu.t[:, 0:n], pu.r[0]), V(wu.t[:, k, f * 128:(f + 1) * 128], wu.r[0]), V(xn.t[:, k, 0:n], xn.r[k]),
                       start=(k == 0), stop=(k == 7))
                sg = tmp()
                act(V(sg.t[:, 0:n], sg.r[0]), V(pg.t[:, 0:n], pg.r[0]), AF.Silu)
                tt('dve', V(hid.t[:, jt + f, 0:n], hid.r[jt + f]), V(pu.t[:, 0:n], pu.r[0]), V(sg.t[:, 0:n], sg.r[0]), ALU.mult)

        def ev(m, psv):
            stt(V(h.t[:, m, 0:n], h.r[m]), psv, 0.5, V(h.t[:, m, 0:n], h.r[m]), ALU.mult, ALU.add)
        linear(w_d, NF, 0, 1024, hid, n, ev)

    def load_tokens(dram_rows, dst, ncols_tok, nchunk_feat, stage_bufs, to_bf16, npre=0):
        ntok = dram_rows.shape[0]
        for i in range(ntok // 128):
            st = stage_bufs[i % 2]
            F = nchunk_feat * 128
            if i >= npre:
                dma('sp', V(st.t[:, 0:F], st.r[0]), V(dram_rows[i * 128:(i + 1) * 128, :]))
            for k0 in range(0, nchunk_feat, 4):
                nk = min(4, nchunk_feat - k0)
                ps = psum()
                for k in range(nk):
                    tr(V(ps.t[:, k * 128:(k + 1) * 128], ps.r[0]), V(st.t[:, (k0 + k) * 128:(k0 + k + 1) * 128], st.r[0]), Vb(ident_f))
                eng = 'act' if (k0 // 4) % 2 == 0 else 'dve'
                cp(eng, V(dst.t[:, k0:k0 + nk, i * 128:(i + 1) * 128], *dst.r[k0:k0 + nk]),
                   V(ps.t[:, 0:nk * 128].rearrange("p (k t) -> p k t", k=nk), ps.r[0]))

    def prefetch_x(dram_rows):
        ntile = min(2, dram_rows.shape[0] // 128)
        for i in range(ntile):
            dma('sp', V(xst[i].t[:, 0:1024], xst[i].r[0]), V(dram_rows[i * 128:(i + 1) * 128, :]))
        return ntile

    def store_tokens(src, dram_rows, n):
        for i in range(n // 128):
            for k0 in range(0, 8, 4):
                ps = psum()
                for k in range(4):
                    tr(V(ps.t[:, k * 128:(k + 1) * 128], ps.r[0]), V(src.t[:, k0 + k, i * 128:(i + 1) * 128], src.r[k0 + k]), Vb(ident_f))
                st = tmp()
                cp('act' if k0 == 0 else 'dve', V(st.t[:, 0:512], st.r[0]), V(ps.t[:, :], ps.r[0]))
                dma('sp', V(dram_rows[i * 128:(i + 1) * 128, k0 * 128:(k0 + 4) * 128]), V(st.t[:, 0:512], st.r[0]), is_out=True)

    PV = max(15 + T, 16 * 23)
    assert PV == 527
    vb = carve(B0, [128, 8, PV], F32)
    pw = [carve(B0 + 16864 + i * 4216, [128, 2, PV], F32) for i in range(2)]
    for pw_ in pw:
        pw_.r = [tuple(r_ for t_ in pw_.r for r_ in t_)]
    cso = sb("cso", [48, 4, 128])
    cst0 = carve(B0 + 32768, [128, 4, 128], F32)
    cst0.r = [tuple(r_ for t_ in cst0.r for r_ in t_)]
    S0h = [carve(B0 + 8192 + i * 8192, [128, 8, 2, 128], F32) for i in range(2)]
    for s0_ in S0h:
        s0_.r = [tuple(r_ for t_ in s0_.r for r_ in t_)]
    s_alias = Buf(S.t[:, :, :].rearrange("p (b j) n -> p b j n", j=2), 0)
    s_alias.r = [tuple(S.r)]
    S0h.append(s_alias)

    def load_state(q):
        g_, hb = q // 2, q % 2
        buf = S0h[q % 3]
        for j in range(2):
            dma('sp', V(buf.t[:, :, j, :], buf.r[0]),
                V(ssm0[hb * 8:(hb + 1) * 8, 4 * g_ + 2 * j:4 * g_ + 2 * j + 2].rearrange("b hh p n -> (hh p) b n")))
    STs = carve(B0 + 24576, [128, 16, 256], BF16)
    STs.r = [tuple(r_ for t_ in STs.r for r_ in t_)]
    Cm = carve(18432, [128, 16, 128], BF16)
    Cm.r = [tuple(r_ for t_ in Cm.r for r_ in t_)]
    BmA = sb("BmA", [128, 16, 128], BF16)
    bm_rr = [0]

    def bcast_mid(ap2, nb):
        return ap2.unsqueeze(1).to_broadcast([ap2.shape[0], nb, ap2.shape[1]])

    def mixer(n, sample, first, last):
        nch = n // 128
        stop_ = STOP_S if sample else STOP_AT
        pre, cv, y2 = (pre_s, cv_s, y2_s) if sample else (None, None, y2_p)
        XC = [xc_s, xc_s] if sample else xc_pp
        SZ = [sz_s, sz_s] if sample else sz_pp
        rmsnorm(h, 8, 1024, "g_mix", 0, xn, n)
        u = xn
        m01 = m01s if sample else m01p
        seg = 8 if sample else 128
        nseg = n // seg
        ps = psum()
        for k in range(8):
            mm(V(ps.t[0:32, 0:n], ps.r[0]), V(wdt.t[:, k, :], wdt.r[0]), V(u.t[:, k, 0:n], u.r[k]), start=(k == 0), stop=(k == 7))
        d = {k: V(b.t[:, 0:n], b.r[0]) for k, b in dtb.items()}
        act(d["dt"], V(ps.t[0:32, 0:n], ps.r[0]), AF.Exp, bias=V(s32_t.t[:, 0:1], s32_t.r[0]))
        act(d["dt"], d["dt"], AF.Ln, bias=V(one_t.t[0:32, :], one_t.r[0]))
        ts('dve', d["dtA"], d["dt"], Vb(A32), None, ALU.mult)
        cmk = cms if sample else cmp_
        P.add('dve', lambda e: e.tensor_tensor_scan(d["acs"].ap, cmk.t[:, 0:n], d["dtA"].ap, 0.0, ALU.mult, ALU.add),
              [cmk.r[0], d["dtA"]], [d["acs"]])
        act(d["lb"], d["dt"], AF.Ln)
        tt('dve', d["lb"], d["lb"], d["acs"], ALU.subtract)
        act(d["ea"], d["acs"], AF.Exp)
        acs3 = dtb["acs"].t[:, 0:n].rearrange("p (c t) -> p c t", t=seg)
        dtds3 = dtb["dtds"].t[:, 0:n].rearrange("p (c t) -> p c t", t=seg)
        P.add('dve', lambda e: e.tensor_tensor(dtds3, acs3[:, :, seg - 1:seg].to_broadcast([32, nseg, seg]), acs3, ALU.subtract),
              [d["acs"]], [d["dtds"]])
        act(d["dtds"], d["dtds"], AF.Exp)
        tt('dve', d["dtds"], d["dtds"], d["dt"], ALU.mult)
        ps = psum()
        for c in range(nch):
            for j, k in enumerate(("lb", "ea", "dtds")):
                tr(V(ps.t[:, (c * 3 + j) * 32:(c * 3 + j + 1) * 32], ps.r[0]), V(dtb[k].t[:, c * 128:(c + 1) * 128], dtb[k].r[0]),
                   V(ident_f.t[0:32, 0:32], ident_f.r[0]))
        cp('dve', V(tm32.t[:, 0:nch, :, :], tm32.r[0]),
           V(ps.t[:, 0:nch * 96].rearrange("p (c j h) -> p c j h", c=nch, j=3), ps.r[0]))
        ea_last = dtb["ea"].t[:, 0:n].rearrange("p (c t) -> p c t", t=seg)[:, :, seg - 1]
        ps = psum()
        for j in range(16):
            lhs = selp.t[:, j, :]
            P.add('pe', (lambda lhs, j, ps: lambda e: e.matmul(ps.t[:, j * nseg:(j + 1) * nseg], lhs, ea_last, start=True, stop=True))(lhs, j, ps),
                  [selp.r[0], d["ea"]], [ps.r[0]])
        cp('act', V(cdpm.t[:, :, 0:nseg], cdpm.r[0]), V(ps.t[:, 0:16 * nseg].rearrange("p (j c) -> p j c", j=16), ps.r[0]))

        if stop_ <= 1:
            return
        def vview(ap3, lo, hi):
            if sample:
                return ap3[:, :, 0:16 * 23].rearrange("p k (b t) -> p k b t", t=23)[:, :, :, lo:hi]
            return ap3[:, :, lo:hi]
        L = 23 if sample else 15 + n
        if not sample:
            for k in range(8):
                if first:
                    memset('pool', V(vb.t[:, k, 0:15], vb.r[k]), 0.0)
                else:
                    cp('pool', V(vb.t[:, k, 0:15], vb.r[k]), V(ptail.t[:, k, :], ptail.r[k]))
        else:
            for half in range(2):
                st = xst[xst_rr[0] % 2]; xst_rr[0] += 1
                dma('sp', V(st.t[0:120, 0:1024], st.r[0]), V(pool0[half * 8:(half + 1) * 8].rearrange("b t c -> (b t) c")))
                for k0 in (0, 4):
                    ps = psum()
                    for k in range(4):
                        tr(V(ps.t[:, k * 128:k * 128 + 120], ps.r[0]), V(st.t[0:120, (k0 + k) * 128:(k0 + k + 1) * 128], st.r[0]),
                           V(ident_f.t[0:120, 0:120], ident_f.r[0]))
                    for k in range(4):
                        cp('act' if k % 2 else 'dve',
                           V(vb.t[:, k0 + k, 0:16 * 23].rearrange("p (b t) -> p b t", t=23)[:, half * 8:(half + 1) * 8, 0:15], vb.r[k0 + k]),
                           V(ps.t[:, k * 128:k * 128 + 120].rearrange("p (b t) -> p b t", t=15), ps.r[0]))

        def ev_v(m, psv):
            if sample:
                cp('act', V(vb.t[:, m, 0:16 * 23].rearrange("p (b t) -> p b t", t=23)[:, :, 15:23], vb.r[m]),
                   V(psv.ap.rearrange("p (b t) -> p b t", t=8), *psv.res))
                cp('dve', V(vs.t[:, m, 0:128], vs.r[m]), psv)
            else:
                cp('act', V(vb.t[:, m, 15:15 + n], vb.r[m]), psv)
        linear(w_in, 8, 6176, 1024, u, n, ev_v)
        for gi, w in enumerate((2, 4, 8, 16)):
            c0 = 2 * gi
            cur = vb.t[:, c0:c0 + 2, :]
            cur_r = [vb.r[c0], vb.r[c0 + 1]]
            sh = 1
            pi = 0
            while sh < w:
                dst = pw[pi % 2]
                pi += 1
                o_ = vview(dst.t[:, :, :], 2 * sh - 1, L); a_ = vview(cur, 2 * sh - 1, L); b_ = vview(cur, sh - 1, L - sh)
                P.add('dve', (lambda o_, a_, b_: lambda e: e.tensor_tensor(o_, a_, b_, ALU.add))(o_, a_, b_), cur_r, [dst.r[0]])
                cur = dst.t[:, :, :]
                cur_r = [dst.r[0]]
                sh *= 2
            for k in range(2):
                sv = vview(cur, 15, L)[:, k]
                vv = vview(vb.t[:, c0:c0 + 2, :], 15, L)[:, k]
                if sample:
                    po_ = pooled.t[:, c0 + k, 0:n].rearrange("p (b t) -> p b t", t=8)
                else:
                    po_ = pooled.t[:, c0 + k, 0:n]
                P.add('dve', (lambda sv, vv, po_, w: lambda e: e.scalar_tensor_tensor(po_, sv, 1.0 / w, vv, ALU.mult, ALU.subtract))(sv, vv, po_, w),
                      cur_r + [vb.r[c0 + k]], [pooled.r[c0 + k]])
                if first and not sample:
                    t16 = tmp()
                    P.add('dve', (lambda k, gi, cur, t16: lambda e: e.tensor_tensor(t16.t[:, 0:16], cur[:, k, 15:31], ic16.t[:, gi * 16:(gi + 1) * 16], ALU.mult))(k, gi, cur, t16),
                          cur_r + [ic16.r[0]], [t16.r[0]])
                    P.add('dve', (lambda k, c0, t16: lambda e: e.tensor_tensor(pooled.t[:, c0 + k, 0:16], t16.t[:, 0:16], vb.t[:, c0 + k, 15:31], ALU.subtract))(k, c0, t16),
                          [t16.r[0], vb.r[c0 + k]], [pooled.r[c0 + k]])
        if sample:
            dma('sp', V(o_pool_s[:, 0:7, :]), V(pool0[:, 8:15, :]), is_out=True)
            st = xst[xst_rr[0] % 2]; xst_rr[0] += 1
            for k0 in (0, 4):
                ps = psum()
                for k in range(4):
                    tr(V(ps.t[:, k * 128:(k + 1) * 128], ps.r[0]), V(vs.t[:, k0 + k, 0:128], vs.r[k0 + k]), Vb(ident_f))
                cp('act' if k0 == 0 else 'dve', V(st.t[:, k0 * 128:(k0 + 4) * 128], st.r[0]), V(ps.t[:, :], ps.r[0]))
            for b in range(16):
                dma('sp', V(o_pool_s[b, 7:15, :]), V(st.t[8 * b:8 * b + 8, :], st.r[0]), is_out=True)
        else:
            for k in range(8):
                cp('pool', V(ptail.t[:, k, :], ptail.r[k]), V(vb.t[:, k, n:n + 15], vb.r[k]))
            if last:
                for k0 in (0, 4):
                    ps = psum()
                    for k in range(4):
                        tr(V(ps.t[0:15, k * 128:(k + 1) * 128], ps.r[0]), V(ptail.t[:, k0 + k, :], ptail.r[k0 + k]), Vb(ident_f))
                    st = tmp()
                    cp('act', V(st.t[0:15, 0:512], st.r[0]), V(ps.t[0:15, 0:512], ps.r[0]))
                    dma('sp', V(o_pool_p[:, k0 * 128:(k0 + 4) * 128]), V(st.t[0:15, 0:512], st.r[0]), is_out=True)
        for gi in range(4):
            wg_ = load_w([(w_pg[gi], 2, 0)])
            for dc in range(2):
                ps = psum()
                for cc in range(2):
                    mm(V(ps.t[:, 0:n], ps.r[0]), V(wg_.t[:, cc, dc * 128:(dc + 1) * 128], wg_.r[0]),
                       V(pooled.t[:, 2 * gi + cc, 0:n], pooled.r[2 * gi + cc]), start=(cc == 0), stop=(cc == 1))
                ts('dve', V(pooled2.t[:, 2 * gi + dc, 0:n], pooled2.r[2 * gi + dc]), V(ps.t[:, 0:n], ps.r[0]), vcol("pscale", 2 * gi + dc), None, ALU.mult)

        if stop_ <= 2:
            return
        def make_group(g):
            idx = [2 * g, 2 * g + 1, 16 + g, 24 + g]
            hs = slice(4 * g, 4 * g + 4)
            ng = negs if sample else negp
            xc = XC[g % 2]
            sz = SZ[g % 2]

            def front():
                if not sample:

                    sA = load_w([(w_in[:, 2048 + 256 * g:2048 + 256 * (g + 1)], 8, 0), (w_in[:, 256 * g:256 * (g + 1)], 8, 256)])
                    sB = load_w([(w_in[:, 4096 + 128 * g:4096 + 128 * (g + 1)], 8, 0), (w_in[:, 5120 + 128 * g:5120 + 128 * (g + 1)], 8, 128)])
                    dg = DG[g % 2]
                    for j in range(4):
                        if first:
                            memset('pool', V(pre16.t[:, j, 0:3], pre16.r[j]), 0.0)
                        else:
                            cp('pool', V(pre16.t[:, j, 0:3], pre16.r[j]), V(ctail.t[:, idx[j], :], ctail.r[idx[j]]))
                    for tap in range(4):
                        for j in range(4):
                            wcol = V(vec_t.t[:, VC["conv_w"] + tap * 32 + idx[j]:VC["conv_w"] + tap * 32 + idx[j] + 1], vec_t.r[0])
                            ts('pool', V(dg.t[:, tap * 4 + j, :], dg.r[0]), Vb(ident_b), wcol, 1.0, ALU.mult, ALU.mult)
                    for j in range(6):
                        slot, col = (sA, j * 128) if j < 2 else ((sB, (j - 2) * 128) if j < 4 else (sA, 256 + (j - 4) * 128))
                        ps = psum()
                        for k in range(8):
                            mm(V(ps.t[:, 0:n], ps.r[0]), V(slot.t[:, k, col:col + 128], slot.r[0]), V(u.t[:, k, 0:n], u.r[k]),
                               start=(k == 0), stop=(k == 7))
                        if j < 4:
                            cp('act', V(pre16.t[:, j, 3:3 + n], pre16.r[j]), V(ps.t[:, 0:n], ps.r[0]))
                            cp('dve', V(ctail.t[:, idx[j], :], ctail.r[idx[j]]), V(ps.t[:, n - 3:n], ps.r[0]))
                        else:
                            act(V(sz.t[:, j - 4, 0:n], sz.r[j - 4]), V(ps.t[:, 0:n], ps.r[0]), AF.Silu)
                    for j in range(4):
                        ps = psum()
                        for tap in range(4):
                            mm(V(ps.t[:, 0:n], ps.r[0]), V(dg.t[:, tap * 4 + j, :], dg.r[0]), V(pre16.t[:, j, tap:tap + n], pre16.r[j]),
                               start=(tap == 0), stop=(tap == 3))
                        act(V(xc.t[:, j, 0:n], xc.r[j]), V(ps.t[:, 0:n], ps.r[0]), AF.Silu, bias=vcol("conv_b", idx[j]))
                    return
                sA = load_w([(w_in[:, 2048 + 256 * g:2048 + 256 * (g + 1)], 8, 0), (w_in[:, 256 * g:256 * (g + 1)], 8, 256)])
                sB = load_w([(w_in[:, 4096 + 128 * g:4096 + 128 * (g + 1)], 8, 0), (w_in[:, 5120 + 128 * g:5120 + 128 * (g + 1)], 8, 128)])

                def pview(j, lo, hi):
                    if sample:
                        return pre.t[:, j, 0:176].rearrange("p (b t) -> p b t", t=11)[:, :, lo:hi]
                    return pre.t[:, j, lo:hi]
                if sample:
                    for j in range(4):
                        dma('sp', V(cst0.t[0:48, j, :], cst0.r[0]), V(conv0[:, idx[j] * 128:(idx[j] + 1) * 128]))
                    ps = psum()
                    for j in range(4):
                        tr(V(ps.t[:, j * 48:(j + 1) * 48], ps.r[0]), V(cst0.t[0:48, j, :], cst0.r[0]), V(ident_f.t[0:48, 0:48], ident_f.r[0]))
                    for j in range(4):
                        cp('dve', V(pview(j, 0, 3), pre.r[j]), V(ps.t[:, j * 48:(j + 1) * 48].rearrange("p (b t) -> p b t", t=3), ps.r[0]))
                else:
                    for j in range(4):
                        if first:
                            memset('pool', V(pview(j, 0, 3), pre.r[j]), 0.0)
                        else:
                            cp('pool', V(pview(j, 0, 3), pre.r[j]), V(ctail.t[:, idx[j], :], ctail.r[idx[j]]))
                for j in range(6):
                    slot, col = (sA, j * 128) if j < 2 else ((sB, (j - 2) * 128) if j < 4 else (sA, 256 + (j - 4) * 128))
                    ps = psum()
                    for k in range(8):
                        mm(V(ps.t[:, 0:n], ps.r[0]), V(slot.t[:, k, col:col + 128], slot.r[0]), V(u.t[:, k, 0:n], u.r[k]),
                           start=(k == 0), stop=(k == 7))
                    if j < 4:
                        if sample:
                            cp('act', V(pview(j, 3, 11), pre.r[j]), V(ps.t[:, 0:n].rearrange("p (b t) -> p b t", t=8), ps.r[0]))
                            tq = tmp()
                            cp('dve', V(tq.t[:, 0:48].rearrange("p (b t) -> p b t", t=3), tq.r[0]),
                               V(ps.t[:, 0:128].rearrange("p (b t) -> p b t", t=8)[:, :, 5:8], ps.r[0]))
                            ps2 = psum()
                            tr(V(ps2.t[0:48, 0:128], ps2.r[0]), V(tq.t[:, 0:48], tq.r[0]), Vb(ident_f))
                            cp('act', V(cso.t[:, j, :], cso.r[0]), V(ps2.t[0:48, 0:128], ps2.r[0]))
                            dma('sp', V(o_conv_s.rearrange("b t c -> (b t) c")[:, idx[j] * 128:(idx[j] + 1) * 128]), V(cso.t[:, j, :], cso.r[0]), is_out=True)
                        else:
                            cp('act', V(pview(j, 3, 3 + n), pre.r[j]), V(ps.t[:, 0:n], ps.r[0]))
                    else:
                        act(V(sz.t[:, j - 4, 0:n], sz.r[j - 4]), V(ps.t[:, 0:n], ps.r[0]), AF.Silu)
                if not sample:
                    for j in range(4):
                        cp('pool', V(ctail.t[:, idx[j], :], ctail.r[idx[j]]), V(pre.t[:, j, n:n + 3], pre.r[j]))
                for tap in range(4):
                    for j in range(4):
                        if sample:
                            src = V(pview(j, tap, tap + 8), pre.r[j])
                            dstv = V(cv.t[:, j, 0:n].rearrange("p (b t) -> p b t", t=8), cv.r[j])
                        else:
                            src = V(pview(j, tap, tap + n), pre.r[j])
                            dstv = V(cv.t[:, j, 0:n], cv.r[j])
                        wcol = V(vec_t.t[:, VC["conv_w"] + tap * 32 + idx[j]:VC["conv_w"] + tap * 32 + idx[j] + 1], vec_t.r[0])
                        if tap == 0:
                            ts('dve', dstv, src, wcol, vcol("conv_b", idx[j]), ALU.mult, ALU.add)
                        else:
                            stt(dstv, src, wcol, dstv, ALU.mult, ALU.add)
                for j in range(4):
                    act(V(xc.t[:, j, 0:n], xc.r[j]), V(cv.t[:, j, 0:n], cv.r[j]), AF.Silu)

                if sample:
                    if g == 0:
                        load_state(0)
                        load_state(1)
                    for b in range(16):
                        s0 = S0h[(2 * g + b // 8) % 3]
                        ps = psum()
                        for j in range(2):
                            tr(V(ps.t[:, j * 128:(j + 1) * 128], ps.r[0]), V(s0.t[:, b % 8, j, :], s0.r[0]), Vb(ident_f))
                        cp('act' if b % 2 else 'dve', V(STs.t[:, b, :], STs.r[0]), V(ps.t[:, 0:256], ps.r[0]))
                    memset('pool', V(Cm.t[:, :, :], Cm.r[0]), 0.0)
                    for b in range(16):
                        cp('pool', V(Cm.t[:, b, 8 * b:8 * b + 8], Cm.r[0]), V(xc.t[:, 3, 8 * b:8 * b + 8], xc.r[3]))


            def p1():
                for c0_ in range(0, nch, 2):
                    nc2 = min(2, nch - c0_)
                    ps = psum()
                    for cc in range(nc2):
                        cs = slice((c0_ + cc) * 128, (c0_ + cc + 1) * 128)
                        for j in range(2):
                            mm(V(ps.t[:, cc * 256 + j * 128:cc * 256 + (j + 1) * 128], ps.r[0]), V(xc.t[:, j, cs], xc.r[j]), Vb(ident_b))
                    cp('act', V(ckb["x_tm"].t[:, c0_:c0_ + nc2, :], ckb["x_tm"].r[0]),
                       V(ps.t[:, 0:nc2 * 256].rearrange("p (c d) -> p c d", c=nc2), ps.r[0]))
                    P.add('dve', (lambda ps, c0_, nc2, hs: lambda e: e.tensor_tensor(
                        ckb["xw"].t[:, c0_:c0_ + nc2, :].rearrange("p c (h d) -> p c h d", h=4),
                        ps.t[:, 0:nc2 * 256].rearrange("p (c h d) -> p c h d", c=nc2, h=4),
                        tm32.t[:, c0_:c0_ + nc2, 2, hs].unsqueeze(3).to_broadcast([128, nc2, 4, 64]), ALU.mult))(ps, c0_, nc2, hs),
                        [ps.r[0], tm32.r[0]], [ckb["xw"].r[0]])
                ps = psum()
                for c in range(nch):
                    cs = slice(c * 128, (c + 1) * 128)
                    mm(V(ps.t[:, cs], ps.r[0]), V(xc.t[:, 2, cs], xc.r[2]), Vb(ident_b))
                cp('act', V(ckb["B_tm"].t[:, 0:nch, :], ckb["B_tm"].r[0]), V(ps.t[:, 0:n].rearrange("p (c d) -> p c d", c=nch), ps.r[0]))
                ps = psum()
                for c in range(nch):
                    cs = slice(c * 128, (c + 1) * 128)
                    mm(V(ps.t[:, cs], ps.r[0]), V(xc.t[:, 2, cs], xc.r[2]), V(xc.t[:, 3, cs], xc.r[3]))
                P.add('dve', (lambda ps, m01: lambda e: e.tensor_tensor(
                    ckb["CBm"].t[:, 0:nch, :], ps.t[:, 0:n].rearrange("p (c d) -> p c d", c=nch),
                    m01.t[:, :].unsqueeze(1).to_broadcast([128, nch, 128]), ALU.mult))(ps, m01),
                    [ps.r[0], m01.r[0]], [ckb["CBm"].r[0]])
                for r_ in range(4):
                    hh = 4 * g + r_
                    ps = psum()
                    mm(V(ps.t[:, 0:n], ps.r[0]), Vb(ident_b), V(ng.t[:, 0:n], ng.r[0]), start=True, stop=False)
                    lhs = ident_f.t[0:32, hh:hh + 1].to_broadcast([32, 128])
                    P.add('pe', (lambda ps, lhs: lambda e: e.matmul(ps.t[:, 0:n], lhs, dtb["acs"].t[:, 0:n], start=False, stop=True))(ps, lhs),
                          [ident_f.r[0], dtb["acs"].r[0]], [ps.r[0]])
                    for c in range(nch):
                        cs = slice(c * 128, (c + 1) * 128)
                        act(V(ckb["W"].t[:, r_, c, :], ckb["W"].r[r_]), V(ps.t[:, cs], ps.r[0]), AF.Exp,
                            bias=V(tm32.t[:, c, 0, hh:hh + 1], tm32.r[0]))
                    tt('dve', V(ckb["W"].t[:, r_, 0:nch, :], ckb["W"].r[r_]), V(ckb["W"].t[:, r_, 0:nch, :], ckb["W"].r[r_]),
                       V(ckb["CBm"].t[:, 0:nch, :], ckb["CBm"].r[0]), ALU.mult)

            def p23():
                for c in range(nch):
                    cs = slice(c * 128, (c + 1) * 128)
                    psd = psum()
                    for r_ in range(4):
                        mm(V(psd.t[:, r_ * 64:(r_ + 1) * 64], psd.r[0]), V(ckb["W"].t[:, r_, c, :], ckb["W"].r[r_]),
                           V(ckb["x_tm"].t[:, c, r_ * 64:(r_ + 1) * 64], ckb["x_tm"].r[0]))
                    pso = psum()
                    if sample:
                        for b in range(16):
                            mm(V(pso.t[:, 0:256], pso.r[0]), V(Cm.t[:, b, :], Cm.r[0]), V(STs.t[:, b, :], STs.r[0]), start=(b == 0), stop=(b == 15))
                    else:
                        mm(V(pso.t[:, 0:256], pso.r[0]), V(xc.t[:, 3, cs], xc.r[3]), V(ST16.t[:, g, :], ST16.r[g]))
                    yo_ = ckb["yo"][c % 2]
                    P.add('dve', (lambda yo_, pso, c, hs: lambda e: e.tensor_tensor(
                        yo_.t[:, :].rearrange("p (h d) -> p h d", h=4), pso.t[:, 0:256].rearrange("p (h d) -> p h d", h=4),
                        tm32.t[:, c, 1, hs].unsqueeze(2).to_broadcast([128, 4, 64]), ALU.mult))(yo_, pso, c, hs),
                        [pso.r[0], tm32.r[0]], [yo_.r[0]])
                    tt('dve', V(ckb["y1"].t[:, c, :], ckb["y1"].r[0]), V(psd.t[:, 0:256], psd.r[0]), Vb(yo_), ALU.add)
                    if sample:
                        P.add('dve', (lambda c: lambda e: e.tensor_tensor(
                            BmA.t[:, :, :], ckb["B_tm"].t[:, c, :].unsqueeze(1).to_broadcast([128, 16, 128]),
                            rowsel.t[:, 0:16].unsqueeze(2).to_broadcast([128, 16, 128]), ALU.mult))(c),
                            [ckb["B_tm"].r[0], rowsel.r[0]], [BmA.r[0]])
                        for hb in range(2):
                            q = 2 * g + hb
                            s0 = S0h[q % 3]
                            if q + 2 < 16:
                                load_state(q + 2)
                            P.add('dve', (lambda g, s0, hb: lambda e: e.tensor_tensor(
                                s0.t[:, :, :, :], s0.t[:, :, :, :],
                                cdpm.t[:, 2 * g:2 * g + 2, hb * 8:(hb + 1) * 8].rearrange("p j b -> p b j").unsqueeze(3).to_broadcast([128, 8, 2, 128]),
                                ALU.mult))(g, s0, hb), [s0.r[0], cdpm.r[0]], [s0.r[0]])
                            for b0_ in range(0, 8, 2):
                                ps = psum()
                                for bb in range(2):
                                    for j in range(2):
                                        mm(V(ps.t[:, (bb * 2 + j) * 128:(bb * 2 + j + 1) * 128], ps.r[0]),
                                           V(ckb["xw"].t[:, c, j * 128:(j + 1) * 128], ckb["xw"].r[0]),
                                           V(BmA.t[:, hb * 8 + b0_ + bb, :], BmA.r[0]))
                                P.add('dve', (lambda s0, ps, b0_: lambda e: e.tensor_tensor(
                                    s0.t[:, b0_:b0_ + 2, :, :], s0.t[:, b0_:b0_ + 2, :, :],
                                    ps.t[:, 0:512].rearrange("p (b j n) -> p b j n", b=2, j=2), ALU.add))(s0, ps, b0_),
                                    [s0.r[0], ps.r[0]], [s0.r[0]])
                            for j in range(2):
                                dma('sp', V(o_ssm_s[hb * 8:(hb + 1) * 8, 4 * g + 2 * j:4 * g + 2 * j + 2].rearrange("b hh p n -> (hh p) b n")),
                                    V(s0.t[:, :, j, :], s0.r[0]), is_out=True)
                    else:
                        ps = psum()
                        for j in range(2):
                            mm(V(ps.t[:, j * 128:(j + 1) * 128], ps.r[0]), V(ckb["xw"].t[:, c, j * 128:(j + 1) * 128], ckb["xw"].r[0]),
                               V(ckb["B_tm"].t[:, c, :], ckb["B_tm"].r[0]))
                        for j in range(2):
                            stt(V(S.t[:, 2 * g + j, :], S.r[2 * g + j]), V(S.t[:, 2 * g + j, :], S.r[2 * g + j]),
                                V(cdpm.t[:, 2 * g + j, c:c + 1], cdpm.r[0]), V(ps.t[:, j * 128:(j + 1) * 128], ps.r[0]), ALU.mult, ALU.add)
                        if not (last and c == nch - 1):
                            ps = psum()
                            for j in range(2):
                                tr(V(ps.t[:, j * 128:(j + 1) * 128], ps.r[0]), V(S.t[:, 2 * g + j, :], S.r[2 * g + j]), Vb(ident_f))
                            cp('act', V(ST16.t[:, g, :], ST16.r[g]), V(ps.t[:, 0:256], ps.r[0]))
                for j in range(2):
                    ps = psum()
                    for c in range(nch):
                        tr(V(ps.t[:, c * 128:(c + 1) * 128], ps.r[0]), V(ckb["y1"].t[:, c, j * 128:(j + 1) * 128], ckb["y1"].r[0]), Vb(ident_f))
                    stt(V(y2.t[:, j, 0:n], y2.r[j]), V(xc.t[:, j, 0:n], xc.r[j]), vcol("dskip", 2 * g + j), V(ps.t[:, 0:n], ps.r[0]),
                        ALU.mult, ALU.add)
                    tt('dve', V(y2.t[:, j, 0:n], y2.r[j]), V(y2.t[:, j, 0:n], y2.r[j]), V(sz.t[:, j, 0:n], sz.r[j]), ALU.mult)
                ya = Buf(y_all.t[:, 2 * g:2 * g + 2, :], 0)
                ya.r = y_all.r[2 * g:2 * g + 2]
                rmsnorm(y2, 2, 256, "g_ssd", 2 * g, ya, n, sq=sq2)


            return front, p1, p23

        groups = [make_group(g) for g in range(8)]
        if sample:
            for (f_, p1_, p23_) in groups:
                f_(); p1_(); p23_()
        else:
            groups[0][0]()
            for g in range(8):
                groups[g][1]()
                if g + 1 < 8:
                    groups[g + 1][0]()
                groups[g][2]()

        if (not sample) and last:
            dma('sp', V(o_ssm_p.rearrange("(j hh) p n -> (hh p) j n", hh=2)), V(S.t[:, :, :], *S.r), is_out=True)
            for k0 in range(0, 32, 4):
                ps = psum()
                for k in range(4):
                    tr(V(ps.t[0:3, k * 128:(k + 1) * 128], ps.r[0]), V(ctail.t[:, k0 + k, :], ctail.r[k0 + k]), Vb(ident_f))
                st = tmp()
                cp('dve', V(st.t[0:3, 0:512], st.r[0]), V(ps.t[0:3, 0:512], ps.r[0]))
                dma('sp', V(o_conv_p[:, k0 * 128:(k0 + 4) * 128]), V(st.t[0:3, 0:512], st.r[0]), is_out=True)

        if stop_ <= 4 or stop_ == 29:
            return
        for ct in range(2):
            wso = [load_w([(w_so[k0 * 128:(k0 + 8) * 128, ct * 512:(ct + 1) * 512], 8, 0)]) for k0 in (0, 8)]
            wg0 = load_w([(w_in[:, 7200 + ct * 512:7200 + (ct + 1) * 512], 8, 0)])
            for m in range(4):
                mi = ct * 4 + m
                pa = psum()
                for k in range(16):
                    mm(V(pa.t[:, 0:n], pa.r[0]), V(wso[k // 8].t[:, k % 8, m * 128:(m + 1) * 128], wso[k // 8].r[0]),
                       V(y_all.t[:, k, 0:n], y_all.r[k]), start=(k == 0), stop=(k == 15))
                pg = psum()
                for k in range(8):
                    mm(V(pg.t[:, 0:n], pg.r[0]), V(wg0.t[:, k, m * 128:(m + 1) * 128], wg0.r[0]), V(u.t[:, k, 0:n], u.r[k]),
                       start=(k == 0), stop=(k == 7))
                s0_ = tmp()
                act(V(s0_.t[:, 0:n], s0_.r[0]), V(pg.t[:, 0:n], pg.r[0]), AF.Sigmoid)
                tt('dve', V(gs.t[:, mi, 0:n], gs.r[mi]), V(pa.t[:, 0:n], pa.r[0]), V(s0_.t[:, 0:n], s0_.r[0]), ALU.mult)
            wpo = load_w([(w_po[:, ct * 512:(ct + 1) * 512], 8, 0)])
            wg1 = load_w([(w_in[:, 8224 + ct * 512:8224 + (ct + 1) * 512], 8, 0)])
            for m in range(4):
                mi = ct * 4 + m
                pb = psum()
                for k in range(8):
                    mm(V(pb.t[:, 0:n], pb.r[0]), V(wpo.t[:, k, m * 128:(m + 1) * 128], wpo.r[0]), V(pooled2.t[:, k, 0:n], pooled2.r[k]),
                       start=(k == 0), stop=(k == 7))
                pg = psum()
                for k in range(8):
                    mm(V(pg.t[:, 0:n], pg.r[0]), V(wg1.t[:, k, m * 128:(m + 1) * 128], wg1.r[0]), V(u.t[:, k, 0:n], u.r[k]),
                       start=(k == 0), stop=(k == 7))
                s1_ = tmp()
                act(V(s1_.t[:, 0:n], s1_.r[0]), V(pg.t[:, 0:n], pg.r[0]), AF.Sigmoid)
                t1_ = tmp()
                tt('dve', V(t1_.t[:, 0:n], t1_.r[0]), V(pb.t[:, 0:n], pb.r[0]), V(s1_.t[:, 0:n], s1_.r[0]), ALU.mult)
                tt('dve', V(merged.t[:, mi, 0:n], merged.r[mi]), V(t1_.t[:, 0:n], t1_.r[0]), V(gs.t[:, mi, 0:n], gs.r[mi]), ALU.add)

        def ev_o(m, psv):
            tt('dve', V(h.t[:, m, 0:n], h.r[m]), psv, V(h.t[:, m, 0:n], h.r[m]), ALU.add)
        linear(w_o, 8, 0, 1024, merged, n, ev_o)

    def ple(p_rows, n):
        rmsnorm(h, 8, 1024, "g_ple", 0, xn, n)
        load_tokens(p_rows, p_fm, n, 2, pst, True)
        for ct in range(2):
            wg = load_w([(w_plg[:, ct * 512:(ct + 1) * 512], 8, 0)])
            wp = load_w([(w_ple[:, ct * 512:(ct + 1) * 512], 2, 0)])
            for m in range(4):
                mi = ct * 4 + m
                pg = psum()
                for k in range(8):
                    mm(V(pg.t[:, 0:n], pg.r[0]), V(wg.t[:, k, m * 128:(m + 1) * 128], wg.r[0]), V(xn.t[:, k, 0:n], xn.r[k]),
                       start=(k == 0), stop=(k == 7))
                pe_ = psum()
                for k in range(2):
                    mm(V(pe_.t[:, 0:n], pe_.r[0]), V(wp.t[:, k, m * 128:(m + 1) * 128], wp.r[0]), V(p_fm.t[:, k, 0:n], p_fm.r[k]),
                       start=(k == 0), stop=(k == 1))
                s_ = tmp()
                act(V(s_.t[:, 0:n], s_.r[0]), V(pg.t[:, 0:n], pg.r[0]), AF.Sigmoid)
                t_ = tmp()
                tt('dve', V(t_.t[:, 0:n], t_.r[0]), V(pe_.t[:, 0:n], pe_.r[0]), V(s_.t[:, 0:n], s_.r[0]), ALU.mult)
                tt('dve', V(h.t[:, mi, 0:n], h.r[mi]), V(h.t[:, mi, 0:n], h.r[mi]), V(t_.t[:, 0:n], t_.r[0]), ALU.add)

    for k in range(16):
        memset('pool', V(S.t[:, k, :], S.r[k]), 0.0)
    for g in range(8):
        memset('pool', V(ST16.t[:, g, :], ST16.r[g]), 0.0)

    blk_ctr = [0]

    npre_x = [0]

    def block(x_rows, p_rows, y_rows, n, sample, first, last, next_x=None):
        cur_blk[0] = blk_ctr[0]
        blk_ctr[0] += 1
        w_idx[0] = 0
        load_tokens(x_rows, h, n, 8, xst, False, npre=npre_x[0])
        npre_x[0] = prefetch_x(next_x) if next_x is not None else 0
        ffn(w_gu1, w_d1, "g_ffn1", n)
        mixer(n, sample, first, last)
        ffn(w_gu2, w_d2, "g_ffn2", n)
        ple(p_rows, n)
        rmsnorm(h, 8, 1024, "g_final", 0, yfin, n)
        store_tokens(yfin, y_rows, n)

    nblk = NBLK if PROMPT_BLOCKS is None else PROMPT_BLOCKS
    for blk in range(nblk):
        rows = slice(blk * T, (blk + 1) * T)
        if blk + 1 < nblk:
            nxt = xp[(blk + 1) * T:(blk + 2) * T, :]
        else:
            nxt = xs if DO_SAMPLE else None
        block(xp[rows, :], pp[rows, :], yp[rows, :], T, False, blk == 0, blk == NBLK - 1, next_x=nxt)
    if DO_SAMPLE:
        block(xs, psm, ys, 128, True, False, False)

    sems = {e: es.enter_context(nc.semaphore("s_" + e)) for e in ('pe', 'act', 'dve', 'pool')}
    dsems = {q: [es.enter_context(nc.semaphore("d_%s%d" % (q, j))) for j in range(NS_DMA)] for q in ('sp', 'pool', 'act')}
    P.emit(nc, sems, dsems)
    es.close()
    return nc


_NC = None


def _pack_vec(i):
    def col(v, n):
        return np.asarray(v, np.float32).reshape(n, 128).T
    cols = [col(i['norm_ffn1'][0], 8), col(i['norm_mix'][0], 8), col(i['norm_ffn2'][0], 8), col(i['norm_ple'][0], 8),
            col(i['norm_final'], 8), col(i['pool_scale'][0], 8), col(i['norm_ssd'][0], 16), col(i['conv_b'][0], 32)]
    cw = np.asarray(i['conv_w'][0], np.float32)
    for k in range(4):
        cols.append(col(cw[k], 32))
    ds = np.asarray(i['d_skip'][0], np.float32)
    dcol = np.repeat(ds, 64).reshape(16, 128).T
    cols.append(dcol)
    return np.ascontiguousarray(np.concatenate(cols, axis=1))


def kernel(**inputs):
    global _NC
    i = {k: np.asarray(v) for k, v in inputs.items()}
    if _NC is None:
        _NC = build()
    nc = _NC
    consts = _consts()
    vec = _pack_vec(i)
    s32 = np.ascontiguousarray(np.stack([i['dt_bias'][0], i['a_log'][0], i['d_skip'][0]], axis=1).astype(np.float32))
    shared = dict(
        w_gu1=i['w_ffn1_gu'][0], w_d1=i['w_ffn1_down'][0], w_gu2=i['w_ffn2_gu'][0], w_d2=i['w_ffn2_down'][0],
        w_in=i['w_in'][0], w_so=i['w_ssd_out'][0], w_pg=i['w_pool_group'][0], w_po=i['w_pool_out'][0], w_o=i['w_o'][0],
        w_plg=i['w_ple_gate'][0], w_ple=i['w_ple'][0], vec=vec, s32=s32)
    for k, v in consts.items():
        shared["c_" + k] = v
    shared = {k: np.ascontiguousarray(v, dtype=np.float32) for k, v in shared.items()}
    in_maps = []
    for c in range(8):
        m = dict(shared)
        m['xp'] = np.ascontiguousarray(i['x_prompt'][c])
        m['xs'] = np.ascontiguousarray(i['x_sample'][16 * c:16 * c + 16].reshape(128, 1024))
        m['ssm0'] = np.ascontiguousarray(i['state_ssm'][0, 16 * c:16 * c + 16])
        m['conv0'] = np.ascontiguousarray(i['state_conv'][0, 16 * c:16 * c + 16].reshape(48, 4096))
        m['pool0'] = np.ascontiguousarray(i['state_pool'][0, 16 * c:16 * c + 16])
        m['pp'] = np.ascontiguousarray(i['p_prompt'][0, c])
        m['psm'] = np.ascontiguousarray(i['p_sample'][0, 16 * c:16 * c + 16].reshape(128, 256))
        in_maps.append(m)
    res = run_bass_kernel_spmd(nc, in_maps, core_ids=list(range(8)))
    rs_ = res.results
    y_prompt = np.stack([r['yp'] for r in rs_], 0)
    y_sample = np.concatenate([r['ys'] for r in rs_], 0).reshape(128, 8, 1024)
    ssm_p = np.stack([r['o_ssm_p'] for r in rs_], 0)[None]
    conv_p = np.stack([r['o_conv_p'] for r in rs_], 0)[None]
    pool_p = np.stack([r['o_pool_p'] for r in rs_], 0)[None]
    ssm_s = np.concatenate([r['o_ssm_s'] for r in rs_], 0)[None]
    conv_s = np.concatenate([r['o_conv_s'] for r in rs_], 0)[None]
    pool_s = np.concatenate([r['o_pool_s'] for r in rs_], 0)[None]
    return (y_prompt, y_sample, ssm_p, conv_p, pool_p, ssm_s, conv_s, pool_s)
```

```python
import numpy as np
from contextlib import ExitStack
import concourse.bass as bass
import concourse.mybir as mybir
from concourse.bass_utils import run_bass_kernel_spmd

F32 = mybir.dt.float32
BF16 = mybir.dt.bfloat16
AF = mybir.ActivationFunctionType
ALU = mybir.AluOpType

T = 512
NCH = T // 128
NBLK = 2048 // T
ENGS = ('pe', 'act', 'dve', 'pool', 'sp')
NS_DMA = 12
EPS = 1e-6
DFF = 2816
NF = DFF // 128
PROMPT_BLOCKS = None
DO_SAMPLE = True
STOP_AT = 99
STOP_S = 99
USE_SCRATCH = True


class R:
    __slots__ = ('w', 'rd', 'psum')

    def __init__(self, psum=False):
        self.w = None
        self.rd = []
        self.psum = psum


class Ins:
    __slots__ = ('eng', 'fn', 'pos', 'dma', 'slot', 'val', 'waits', 'signal')


def _flat(res, out):
    for r in res:
        if r is None:
            continue
        if isinstance(r, (tuple, list)):
            _flat(r, out)
        else:
            out.append(r)
    return out


class V:
    def __init__(self, ap, *res):
        self.ap = ap
        self.res = _flat(res, [])


class Prog:
    def __init__(self):
        self.ins = {e: [] for e in ENGS}
        self.waited = {e: {} for e in ENGS}
        self.dma_rr = {e: 0 for e in ENGS}
        self.dma_last = {e: [None] * NS_DMA for e in ENGS}
        self.dma_cnt = {e: [0] * NS_DMA for e in ENGS}
        self.outs = []

    def add(self, eng, fn, reads, writes, dma=False, is_out=False):
        i = Ins()
        i.eng, i.fn, i.dma, i.signal, i.val, i.slot = eng, fn, dma, False, 0, 0
        i.pos = len(self.ins[eng])
        rr, ww = [], []
        for v in reads:
            rr.extend(v.res if isinstance(v, V) else _flat([v], []))
        for v in writes:
            ww.extend(v.res if isinstance(v, V) else _flat([v], []))
        rr = list(dict.fromkeys(rr))
        ww = list(dict.fromkeys(ww))
        deps = []
        for r in rr:
            if r.w is not None:
                deps.append(r.w)
            if r.psum:
                for x in r.rd:
                    if x.eng != eng:
                        deps.append(x)
        same_ok = (eng == 'pe')
        for r in ww:
            if r.w is not None and (r.w.eng != eng or r.w.dma or dma or not same_ok):
                deps.append(r.w)
            for x in r.rd:
                if x.eng != eng or x.dma or dma or not same_ok:
                    deps.append(x)
        if dma:
            j = self.dma_rr[eng] % NS_DMA
            self.dma_rr[eng] += 1
            if self.dma_last[eng][j] is not None:
                deps.append(self.dma_last[eng][j])
            self.dma_cnt[eng][j] += 16
            i.slot, i.val = j, self.dma_cnt[eng][j]
            self.dma_last[eng][j] = i
        waits = {}
        for d in deps:
            if d is i:
                continue
            key = ('d', d.eng, d.slot) if d.dma else ('c', d.eng)
            v = d.val if d.dma else d.pos
            if self.waited[eng].get(key, -1) >= v:
                continue
            if key not in waits or (waits[key].val if d.dma else waits[key].pos) < v:
                waits[key] = d
        for key, d in waits.items():
            self.waited[eng][key] = d.val if d.dma else d.pos
            d.signal = True
        i.waits = waits
        for r in rr:
            r.rd.append(i)
        for r in ww:
            r.w = i
            r.rd = []
        self.ins[eng].append(i)
        if is_out:
            self.outs.append(i)
        return i

    def emit(self, nc, sems, dsems):
        for e in ENGS:
            c = 0
            for i in self.ins[e]:
                if not i.dma and i.signal:
                    c += 1
                    i.val = c
        outs = self.outs

        def run(e, eng):
            for i in self.ins[e]:
                for key, d in i.waits.items():
                    sem = sems[key[1]] if key[0] == 'c' else dsems[key[1]][key[2]]
                    eng.wait_ge(sem, d.val)
                bi = i.fn(eng)
                if i.dma:
                    bi.then_inc(dsems[e][i.slot], 16)
                elif i.signal:
                    bi.then_inc(sems[e], 1)
            if e == 'sp':
                done = {}
                for d in outs:
                    k = (d.eng, d.slot)
                    done[k] = max(done.get(k, 0), d.val)
                for (q, j), v in done.items():
                    eng.wait_ge(dsems[q][j], v)

        with nc.Block() as block:
            @block.tensor
            def _(pe):
                run('pe', pe)

            @block.scalar
            def _(act):
                run('act', act)

            @block.vector
            def _(dve):
                run('dve', dve)

            @block.gpsimd
            def _(pool):
                run('pool', pool)

            @block.sync
            def _(sp):
                run('sp', sp)


class Buf:
    def __init__(self, t, n):
        self.t = t
        self.r = [R() for _ in range(n)]


def _consts():
    c = {}
    c['ident'] = np.eye(128, dtype=np.float32)
    s = np.arange(128)
    c['m01p'] = (s[None, :] >= s[:, None]).astype(np.float32)
    c['m01s'] = ((s[None, :] >= s[:, None]) & (s[None, :] // 8 == s[:, None] // 8)).astype(np.float32)
    cm = np.ones((32, 512), np.float32)
    cm[:, ::128] = 0.0
    c['cmp'] = cm
    cm = np.ones((32, 128), np.float32)
    cm[:, ::8] = 0.0
    c['cms'] = cm
    ic = np.zeros((128, 4, 16), np.float32)
    for gi, w in enumerate((2, 4, 8, 16)):
        ic[:, gi, :] = 1.0 / np.minimum(np.arange(16) + 1, w)
    c['ic16'] = ic.reshape(128, 64)
    c['negp'] = np.tile(np.where(s[None, :] >= s[:, None], 0.0, -60000.0).astype(np.float32), (1, 4))
    c['negs'] = np.tile(np.where((s[None, :] >= s[:, None]) & (s[None, :] // 8 == s[:, None] // 8), 0.0, -60000.0).astype(np.float32), (1, 4))
    c['rowsel'] = (s[:, None] // 8 == np.arange(16)[None, :]).astype(np.float32)
    bd = (s[None, :] // 8 == np.arange(16)[:, None]).astype(np.float32)
    return c


def build(debug=False):
    nc = bass.Bass("TRN2", target_bir_lowering=False)
    P = Prog()
    es = ExitStack()

    def din(name, shape):
        return nc.dram_tensor(name, list(shape), F32, kind="ExternalInput").ap()

    def dout(name, shape):
        return nc.dram_tensor(name, list(shape), F32, kind="ExternalOutput").ap()

    xp = din("xp", [2048, 1024]); xs = din("xs", [128, 1024])
    ssm0 = din("ssm0", [16, 32, 64, 128]); conv0 = din("conv0", [48, 4096]); pool0 = din("pool0", [16, 15, 1024])
    pp = din("pp", [2048, 256]); psm = din("psm", [128, 256])
    w_gu1 = din("w_gu1", [1024, 2 * DFF]); w_d1 = din("w_d1", [DFF, 1024])
    w_gu2 = din("w_gu2", [1024, 2 * DFF]); w_d2 = din("w_d2", [DFF, 1024])
    w_in = din("w_in", [1024, 9248]); w_so = din("w_so", [2048, 1024])
    w_pg = din("w_pg", [4, 256, 256]); w_po = din("w_po", [1024, 1024]); w_o = din("w_o", [1024, 1024])
    w_plg = din("w_plg", [1024, 1024]); w_ple = din("w_ple", [256, 1024])
    vec = din("vec", [128, 240]); s32 = din("s32", [32, 3])
    cst = {k: din("c_" + k, list(v.shape)) for k, v in _consts().items()}

    yp = dout("yp", [2048, 1024]); ys = dout("ys", [128, 1024])
    o_ssm_p = dout("o_ssm_p", [32, 64, 128]); o_conv_p = dout("o_conv_p", [3, 4096]); o_pool_p = dout("o_pool_p", [15, 1024])
    o_ssm_s = dout("o_ssm_s", [16, 32, 64, 128]); o_conv_s = dout("o_conv_s", [16, 3, 4096]); o_pool_s = dout("o_pool_s", [16, 15, 1024])

    def sb(name, shape, dt=F32, n=1):
        t = es.enter_context(nc.sbuf_tensor("sb_" + name, list(shape), dt))
        return Buf(t, n)

    banks = [Buf(es.enter_context(nc.psum_tensor("bank%d" % i, [128, 512], F32)), 1) for i in range(8)]
    for b_ in banks:
        b_.r[0].psum = True
    bank_rr = [0]

    def psum():
        b = banks[bank_rr[0] % 8]
        bank_rr[0] += 1
        return b

    def mm(out, lhsT, rhs, start=True, stop=True):
        P.add('pe', lambda e: e.matmul(out.ap, lhsT.ap, rhs.ap, start=start, stop=stop), [lhsT, rhs], [out])

    def tr(out, in_, ident):
        P.add('pe', lambda e: e.transpose(out.ap, in_.ap, ident.ap), [in_, ident], [out])

    def act(out, in_, func, bias=None, scale=1.0):
        rd = [in_]
        kw = {}
        if isinstance(bias, V):
            rd.append(bias); kw['bias'] = bias.ap
        elif bias is not None:
            kw['bias'] = bias
        if isinstance(scale, V):
            rd.append(scale); kw['scale'] = scale.ap
        else:
            kw['scale'] = scale
        P.add('act', lambda e: e.activation(out.ap, in_.ap, func, **kw), rd, [out])

    def tt(eng, out, in0, in1, op):
        P.add(eng, lambda e: e.tensor_tensor(out.ap, in0.ap, in1.ap, op), [in0, in1], [out])

    def ts(eng, out, in0, s1, s2, op0, op1=ALU.bypass):
        rd = [in0]
        a1 = s1.ap if isinstance(s1, V) else s1
        a2 = s2.ap if isinstance(s2, V) else s2
        if isinstance(s1, V): rd.append(s1)
        if isinstance(s2, V): rd.append(s2)
        P.add(eng, lambda e: e.tensor_scalar(out.ap, in0.ap, a1, a2, op0, op1), rd, [out])

    def stt(out, in0, sc, in1, op0, op1):
        rd = [in0, in1]
        a = sc.ap if isinstance(sc, V) else sc
        if isinstance(sc, V): rd.append(sc)
        P.add('dve', lambda e: e.scalar_tensor_tensor(out.ap, in0.ap, a, in1.ap, op0, op1), rd, [out])

    def cp(eng, out, in_):
        if eng == 'act':
            P.add('act', lambda e: e.copy(out.ap, in_.ap), [in_], [out])
        else:
            P.add(eng, lambda e: e.tensor_copy(out.ap, in_.ap), [in_], [out])

    def memset(eng, out, c):
        P.add(eng, lambda e: e.memset(out.ap, c), [], [out])

    def dma(q, out, in_, is_out=False, slow=False):
        kw = {'allow_slow_non_contiguous': True} if slow else {}
        P.add(q, lambda e: e.dma_start(out=out.ap, in_=in_.ap, **kw), [in_], [out], dma=True, is_out=is_out)

    ident_f = sb("ident_f", [128, 128]); ident_b = sb("ident_b", [128, 128], BF16); ones_b = sb("ones_b", [128, 128], BF16)
    m01p = sb("m01p", [128, 128]); m01s = sb("m01s", [128, 128])
    cmp_ = sb("cmp", [32, T]); cms = sb("cms", [32, 128]); ic16 = sb("ic16", [128, 64]); rowsel = sb("rowsel", [128, 16])
    vec_t = sb("vec", [128, 240]); s32_t = sb("s32", [32, 3]); A32 = sb("A32", [32, 1])
    wdt = sb("wdt", [128, 8, 32], BF16)
    eps_t = sb("eps", [128, 1]); one_t = sb("one", [128, 1])
    S = sb("S", [128, 16, 128], F32, 16)
    ST16 = sb("ST16", [128, 8, 256], BF16, 8)
    ctail = sb("ctail", [128, 32, 3], F32, 32)
    ptail = sb("ptail", [128, 8, 15], F32, 8)

    def Vb(b, ap=None, i=0):
        return V(b.t[:] if ap is None else ap, b.r[i])

    for nm, b in (("ident", ident_f), ("m01p", m01p), ("m01s", m01s), ("cms", cms), ("ic16", ic16), ("rowsel", rowsel)):
        dma('sp', Vb(b), V(cst[nm]))
    dma('sp', Vb(cmp_), V(cst["cmp"][:, 0:T]))
    dma('sp', Vb(vec_t), V(vec)); dma('sp', Vb(s32_t), V(s32))
    negp = sb("negp", [128, 512], BF16); negs = sb("negs", [128, 512], BF16)
    dma('pool', Vb(negp), V(cst["negp"])); dma('pool', Vb(negs), V(cst["negs"]))
    dma('pool', Vb(ident_b), V(cst["ident"]))
    dma('pool', V(wdt.t[:], wdt.r[0]), V(w_in[:, 6144:6176].rearrange("(kc p) n -> p kc n", p=128)))
    memset('dve', Vb(ones_b), 1.0); memset('dve', Vb(eps_t), EPS); memset('dve', Vb(one_t), 1.0)
    act(Vb(A32), V(s32_t.t[:, 1:2], s32_t.r[0]), AF.Exp)
    ts('dve', Vb(A32), Vb(A32), -1.0, None, ALU.mult)
    selp = sb("selp", [32, 16, 128])
    for j in range(16):
        for hh_ in range(2):
            P.add('dve', (lambda j, hh_: lambda e: e.tensor_copy(selp.t[:, j, hh_ * 64:(hh_ + 1) * 64],
                                                               ident_f.t[0:32, 2 * j + hh_:2 * j + hh_ + 1].to_broadcast([32, 64])))(j, hh_),
                  [ident_f.r[0]], [selp.r[0]])
    VC = dict(g_ffn1=0, g_mix=8, g_ffn2=16, g_ple=24, g_final=32, pscale=40, g_ssd=48, conv_b=64, conv_w=96, dskip=224)

    def vcol(name, c):
        k = VC[name] + c
        return V(vec_t.t[:, k:k + 1], vec_t.r[0])

    NW = 4
    wsl = [sb("wsl%d" % i, [128, 8, 512], BF16) for i in range(NW)]
    w_rr = [0]

    NLOADS = 96
    wscr = nc.dram_tensor("wscr", [NLOADS, 128, 8, 512], BF16).ap()
    wscr_r = [R() for _ in range(NLOADS)]
    cur_blk = [0]
    w_idx = [0]

    def load_w(parts):
        s = wsl[w_rr[0] % NW]
        w_rr[0] += 1
        i = w_idx[0]
        w_idx[0] += 1
        assert i < NLOADS
        nkm = max(nk for (_, nk, _) in parts)
        cm = max(c0 + ap2.shape[1] for (ap2, _, c0) in parts)
        if cur_blk[0] == 0 or not USE_SCRATCH:
            for (ap2, nk, c0) in parts:
                ncol = ap2.shape[1]
                dma('pool', V(s.t[:, 0:nk, c0:c0 + ncol], s.r[0]), V(ap2.rearrange("(kc p) n -> p kc n", p=128)))
            if USE_SCRATCH:
                dma('sp', V(wscr[i, :, 0:nkm, 0:cm], wscr_r[i]), V(s.t[:, 0:nkm, 0:cm], s.r[0]))
        else:
            dma('pool', V(s.t[:, 0:nkm, 0:cm], s.r[0]), V(wscr[i, :, 0:nkm, 0:cm], wscr_r[i]))
        return s

    SEG = 1024
    A_SZ = 23552
    B_SZ = 34816
    arena_t = es.enter_context(nc.sbuf_tensor("sb_arena", [128, (A_SZ + B_SZ) // 2], BF16))
    segs = [R() for _ in range((A_SZ + B_SZ) // SEG)]

    def carve(off, shape, dt):
        esz = 2 if dt == BF16 else 4
        free = 1
        for d_ in shape[1:]:
            free *= d_
        nb = free * esz
        assert off % 4 == 0 and off + nb <= A_SZ + B_SZ, (off, shape)
        ap = arena_t[:, off // 2:(off + nb) // 2]
        if dt != BF16:
            ap = ap.bitcast(dt)
        if len(shape) == 3:
            ap = ap.rearrange("p (a b) -> p a b", a=shape[1])
        elif len(shape) == 4:
            ap = ap.rearrange("p (a b c) -> p a b c", a=shape[1], b=shape[2])
        b = Buf(ap, 0)
        row = nb // shape[1] if len(shape) >= 3 else nb
        nrow = shape[1] if len(shape) >= 3 else 1
        b.r = [tuple(segs[(off + k * row) // SEG:(off + (k + 1) * row - 1) // SEG + 1]) for k in range(nrow)]
        return b
    B0 = A_SZ
    h = sb("h", [128, 8, T], F32, 8)
    xn = sb("xn", [128, 8, T], BF16, 8)
    sq8 = carve(0, [128, 8, T], BF16)
    sq2 = carve(16384, [128, 2, T], BF16)
    rs = [sb("rs%d" % i, [128, T]) for i in range(2)]
    rs_rr = [0]
    hid = carve(0, [128, NF, T], BF16)
    tmpA = [sb("tmpA%d" % i, [128, T]) for i in range(3)]
    tmp_rr = [0]
    xst = [sb("xst%d" % i, [128, 1024]) for i in range(2)]
    xst_rr = [0]
    pst = [sb("pst%d" % i, [128, 256]) for i in range(2)]
    p_fm = carve(B0, [128, 2, T], BF16)
    yfin = carve(0, [128, 8, T], F32)
    gs = carve(B0, [128, 8, T], F32)
    vs = carve(0, [128, 8, 128], F32)
    dtb = {k: sb("dt_" + k, [32, T]) for k in ("dt", "acs", "lb", "ea", "dtds")}
    dtb["dtA"] = dtb["lb"]
    tm32 = sb("tm32", [128, NCH, 3, 32])
    cdpm = sb("cdpm", [128, 16, 16])
    pre16 = carve(B0, [128, 4, 4 + T], BF16)
    pre_s = carve(B0, [128, 4, 176], F32)
    cv_s = carve(B0 + 2816, [128, 4, 128], F32)
    xc_pp = [carve(B0 + 4128 + i * 4096, [128, 4, T], BF16) for i in range(2)]
    xc_s = carve(B0 + 4864, [128, 4, 128], BF16)
    sz_pp = [carve(B0 + 12320 + i * 4096, [128, 2, T], F32) for i in range(2)]
    sz_s = carve(B0 + 5888, [128, 2, 128], F32)
    y2_p = carve(B0 + 20512, [128, 2, T], F32)
    DG = [carve(B0 + 24608 + i * 4096, [128, 16, 128], BF16) for i in range(2)]
    for dg_ in DG:
        dg_.r = [tuple(r_ for t_ in dg_.r for r_ in t_)]
    y2_s = carve(B0 + 6912, [128, 2, 128], F32)
    y_all = carve(0, [128, 16, T], BF16)
    ckb = dict(x_tm=sb("ck_x_tm", [128, NCH, 256], BF16), xw=sb("ck_xw", [128, NCH, 256], BF16),
               B_tm=sb("ck_B_tm", [128, NCH, 128], BF16), CBm=sb("ck_CBm", [128, NCH, 128], F32),
               W=sb("ck_W", [128, 4, NCH, 128], BF16, 4), y1=sb("ck_y1", [128, NCH, 256], F32),
               yo=[sb("ck_yo%d" % i, [128, 256], F32) for i in range(2)])
    pooled = carve(B0 + 25296, [128, 8, T], BF16)
    pooled2 = sb("pooled2", [128, 8, T], BF16, 8)
    merged = carve(B0 + 16384, [128, 8, T], BF16)

    def tmp():
        b = tmpA[tmp_rr[0] % 3]
        tmp_rr[0] += 1
        return b

    def rmsnorm(src, nk, D, gname, gofs, dst, n, sq=None):
        sq = sq if sq is not None else sq8
        for k in range(nk):
            act(V(sq.t[:, k, 0:n], sq.r[k]), V(src.t[:, k, 0:n], src.r[k]), AF.Square)
        ps = psum()
        for k in range(nk):
            mm(V(ps.t[:, 0:n], ps.r[0]), Vb(ones_b), V(sq.t[:, k, 0:n], sq.r[k]), start=(k == 0), stop=(k == nk - 1))
        r = rs[rs_rr[0] % 2]
        rs_rr[0] += 1
        act(V(r.t[:, 0:n], r.r[0]), V(ps.t[:, 0:n], ps.r[0]), AF.Ln, bias=Vb(eps_t), scale=1.0 / D)
        act(V(r.t[:, 0:n], r.r[0]), V(r.t[:, 0:n], r.r[0]), AF.Exp, scale=-0.5)
        for k in range(nk):
            stt(V(dst.t[:, k, 0:n], dst.r[k]), V(src.t[:, k, 0:n], src.r[k]), vcol(gname, gofs + k), V(r.t[:, 0:n], r.r[0]),
                ALU.mult, ALU.mult)

    def linear(w2d, K, c0, ncols, src, n, evac):
        nk_tot = K
        ktiles = [(k0, min(8, nk_tot - k0)) for k0 in range(0, nk_tot, 8)]
        for ct in range(0, ncols, 512):
            nc_ = min(512, ncols - ct)
            nm = nc_ // 128
            pss = [psum() for _ in range(nm)]
            for (k0, nk) in ktiles:
                slot = load_w([(w2d[k0 * 128:(k0 + nk) * 128, c0 + ct:c0 + ct + nc_], nk, 0)])
                for m in range(nm):
                    for kk in range(nk):
                        k = k0 + kk
                        mm(V(pss[m].t[:, 0:n], pss[m].r[0]), V(slot.t[:, kk, m * 128:(m + 1) * 128], slot.r[0]),
                           V(src.t[:, k, 0:n], src.r[k]), start=(k == 0), stop=(k == nk_tot - 1))
            for m in range(nm):
                evac((ct // 128) + m, V(pss[m].t[:, 0:n], pss[m].r[0]))

    def ffn(w_gu, w_d, gname, n):
        rmsnorm(h, 8, 1024, gname, 0, xn, n)
        for jt in range(0, NF, 4):
            nf = min(4, NF - jt)
            wg = load_w([(w_gu[:, jt * 128:(jt + nf) * 128], 8, 0)])
            wu = load_w([(w_gu[:, DFF + jt * 128:DFF + (jt + nf) * 128], 8, 0)])
            for f in range(nf):
                pg = psum(); pu = psum()
                for k in range(8):
                    mm(V(pg.t[:, 0:n], pg.r[0]), V(wg.t[:, k, f * 128:(f + 1) * 128], wg.r[0]), V(xn.t[:, k, 0:n], xn.r[k]),
                       start=(k == 0), stop=(k == 7))
                for k in range(8):
                    mm(V(pu.t[:, 0:n], pu.r[0]), V(wu.t[:, k, f * 128:(f + 1) * 128], wu.r[0]), V(xn.t[:, k, 0:n], xn.r[k]),
                       start=(k == 0), stop=(k == 7))
                sg = tmp()
                act(V(sg.t[:, 0:n], sg.r[0]), V(pg.t[:, 0:n], pg.r[0]), AF.Silu)
                tt('dve', V(hid.t[:, jt + f, 0:n], hid.r[jt + f]), V(pu.t[:, 0:n], pu.r[0]), V(sg.t[:, 0:n], sg.r[0]), ALU.mult)

        def ev(m, psv):
            stt(V(h.t[:, m, 0:n], h.r[m]), psv, 0.5, V(h.t[:, m, 0:n], h.r[m]), ALU.mult, ALU.add)
        linear(w_d, NF, 0, 1024, hid, n, ev)

    def load_tokens(dram_rows, dst, ncols_tok, nchunk_feat, stage_bufs, to_bf16, npre=0):
        ntok = dram_rows.shape[0]
        for i in range(ntok // 128):
            st = stage_bufs[i % 2]
            F = nchunk_feat * 128
            if i >= npre:
                dma('sp', V(st.t[:, 0:F], st.r[0]), V(dram_rows[i * 128:(i + 1) * 128, :]))
            for k0 in range(0, nchunk_feat, 4):
                nk = min(4, nchunk_feat - k0)
                ps = psum()
                for k in range(nk):
                    tr(V(ps.t[:, k * 128:(k + 1) * 128], ps.r[0]), V(st.t[:, (k0 + k) * 128:(k0 + k + 1) * 128], st.r[0]), Vb(ident_f))
                eng = 'act' if (k0 // 4) % 2 == 0 else 'dve'
                cp(eng, V(dst.t[:, k0:k0 + nk, i * 128:(i + 1) * 128], *dst.r[k0:k0 + nk]),
                   V(ps.t[:, 0:nk * 128].rearrange("p (k t) -> p k t", k=nk), ps.r[0]))

    def prefetch_x(dram_rows):
        ntile = min(2, dram_rows.shape[0] // 128)
        for i in range(ntile):
            dma('sp', V(xst[i].t[:, 0:1024], xst[i].r[0]), V(dram_rows[i * 128:(i + 1) * 128, :]))
        return ntile

    def store_tokens(src, dram_rows, n):
        for i in range(n // 128):
            for k0 in range(0, 8, 4):
                ps = psum()
                for k in range(4):
                    tr(V(ps.t[:, k * 128:(k + 1) * 128], ps.r[0]), V(src.t[:, k0 + k, i * 128:(i + 1) * 128], src.r[k0 + k]), Vb(ident_f))
                st = tmp()
                cp('act' if k0 == 0 else 'dve', V(st.t[:, 0:512], st.r[0]), V(ps.t[:, :], ps.r[0]))
                dma('sp', V(dram_rows[i * 128:(i + 1) * 128, k0 * 128:(k0 + 4) * 128]), V(st.t[:, 0:512], st.r[0]), is_out=True)

    PV = max(15 + T, 16 * 23)
    assert PV == 527
    vb = carve(B0, [128, 8, PV], F32)
    pw = [carve(B0 + 16864 + i * 4216, [128, 2, PV], F32) for i in range(2)]
    for pw_ in pw:
        pw_.r = [tuple(r_ for t_ in pw_.r for r_ in t_)]
    cso = sb("cso", [48, 4, 128])
    cst0 = carve(B0 + 32768, [128, 4, 128], F32)
    cst0.r = [tuple(r_ for t_ in cst0.r for r_ in t_)]
    S0h = [carve(B0 + 8192 + i * 8192, [128, 8, 2, 128], F32) for i in range(2)]
    for s0_ in S0h:
        s0_.r = [tuple(r_ for t_ in s0_.r for r_ in t_)]
    s_alias = Buf(S.t[:, :, :].rearrange("p (b j) n -> p b j n", j=2), 0)
    s_alias.r = [tuple(S.r)]
    S0h.append(s_alias)

    def load_state(q):
        g_, hb = q // 2, q % 2
        buf = S0h[q % 3]
        for j in range(2):
            dma('sp', V(buf.t[:, :, j, :], buf.r[0]),
                V(ssm0[hb * 8:(hb + 1) * 8, 4 * g_ + 2 * j:4 * g_ + 2 * j + 2].rearrange("b hh p n -> (hh p) b n")))
    STs = carve(B0 + 24576, [128, 16, 256], BF16)
    STs.r = [tuple(r_ for t_ in STs.r for r_ in t_)]
    Cm = carve(18432, [128, 16, 128], BF16)
    Cm.r = [tuple(r_ for t_ in Cm.r for r_ in t_)]
    BmA = sb("BmA", [128, 16, 128], BF16)
    bm_rr = [0]

    def bcast_mid(ap2, nb):
        return ap2.unsqueeze(1).to_broadcast([ap2.shape[0], nb, ap2.shape[1]])

    def mixer(n, sample, first, last):
        nch = n // 128
        stop_ = STOP_S if sample else STOP_AT
        pre, cv, y2 = (pre_s, cv_s, y2_s) if sample else (None, None, y2_p)
        XC = [xc_s, xc_s] if sample else xc_pp
        SZ = [sz_s, sz_s] if sample else sz_pp
        rmsnorm(h, 8, 1024, "g_mix", 0, xn, n)
        u = xn
        m01 = m01s if sample else m01p
        seg = 8 if sample else 128
        nseg = n // seg
        ps = psum()
        for k in range(8):
            mm(V(ps.t[0:32, 0:n], ps.r[0]), V(wdt.t[:, k, :], wdt.r[0]), V(u.t[:, k, 0:n], u.r[k]), start=(k == 0), stop=(k == 7))
        d = {k: V(b.t[:, 0:n], b.r[0]) for k, b in dtb.items()}
        act(d["dt"], V(ps.t[0:32, 0:n], ps.r[0]), AF.Exp, bias=V(s32_t.t[:, 0:1], s32_t.r[0]))
        act(d["dt"], d["dt"], AF.Ln, bias=V(one_t.t[0:32, :], one_t.r[0]))
        ts('dve', d["dtA"], d["dt"], Vb(A32), None, ALU.mult)
        cmk = cms if sample else cmp_
        P.add('dve', lambda e: e.tensor_tensor_scan(d["acs"].ap, cmk.t[:, 0:n], d["dtA"].ap, 0.0, ALU.mult, ALU.add),
              [cmk.r[0], d["dtA"]], [d["acs"]])
        act(d["lb"], d["dt"], AF.Ln)
        tt('dve', d["lb"], d["lb"], d["acs"], ALU.subtract)
        act(d["ea"], d["acs"], AF.Exp)
        acs3 = dtb["acs"].t[:, 0:n].rearrange("p (c t) -> p c t", t=seg)
        dtds3 = dtb["dtds"].t[:, 0:n].rearrange("p (c t) -> p c t", t=seg)
        P.add('dve', lambda e: e.tensor_tensor(dtds3, acs3[:, :, seg - 1:seg].to_broadcast([32, nseg, seg]), acs3, ALU.subtract),
              [d["acs"]], [d["dtds"]])
        act(d["dtds"], d["dtds"], AF.Exp)
        tt('dve', d["dtds"], d["dtds"], d["dt"], ALU.mult)
        ps = psum()
        for c in range(nch):
            for j, k in enumerate(("lb", "ea", "dtds")):
                tr(V(ps.t[:, (c * 3 + j) * 32:(c * 3 + j + 1) * 32], ps.r[0]), V(dtb[k].t[:, c * 128:(c + 1) * 128], dtb[k].r[0]),
                   V(ident_f.t[0:32, 0:32], ident_f.r[0]))
        cp('dve', V(tm32.t[:, 0:nch, :, :], tm32.r[0]),
           V(ps.t[:, 0:nch * 96].rearrange("p (c j h) -> p c j h", c=nch, j=3), ps.r[0]))
        ea_last = dtb["ea"].t[:, 0:n].rearrange("p (c t) -> p c t", t=seg)[:, :, seg - 1]
        ps = psum()
        for j in range(16):
            lhs = selp.t[:, j, :]
            P.add('pe', (lambda lhs, j, ps: lambda e: e.matmul(ps.t[:, j * nseg:(j + 1) * nseg], lhs, ea_last, start=True, stop=True))(lhs, j, ps),
                  [selp.r[0], d["ea"]], [ps.r[0]])
        cp('act', V(cdpm.t[:, :, 0:nseg], cdpm.r[0]), V(ps.t[:, 0:16 * nseg].rearrange("p (j c) -> p j c", j=16), ps.r[0]))

        if stop_ <= 1:
            return
        def vview(ap3, lo, hi):
            if sample:
                return ap3[:, :, 0:16 * 23].rearrange("p k (b t) -> p k b t", t=23)[:, :, :, lo:hi]
            return ap3[:, :, lo:hi]
        L = 23 if sample else 15 + n
        if not sample:
            for k in range(8):
                if first:
                    memset('pool', V(vb.t[:, k, 0:15], vb.r[k]), 0.0)
                else:
                    cp('pool', V(vb.t[:, k, 0:15], vb.r[k]), V(ptail.t[:, k, :], ptail.r[k]))
        else:
            for half in range(2):
                st = xst[xst_rr[0] % 2]; xst_rr[0] += 1
                dma('sp', V(st.t[0:120, 0:1024], st.r[0]), V(pool0[half * 8:(half + 1) * 8].rearrange("b t c -> (b t) c")))
                for k0 in (0, 4):
                    ps = psum()
                    for k in range(4):
                        tr(V(ps.t[:, k * 128:k * 128 + 120], ps.r[0]), V(st.t[0:120, (k0 + k) * 128:(k0 + k + 1) * 128], st.r[0]),
                           V(ident_f.t[0:120, 0:120], ident_f.r[0]))
                    for k in range(4):
                        cp('act' if k % 2 else 'dve',
                           V(vb.t[:, k0 + k, 0:16 * 23].rearrange("p (b t) -> p b t", t=23)[:, half * 8:(half + 1) * 8, 0:15], vb.r[k0 + k]),
                           V(ps.t[:, k * 128:k * 128 + 120].rearrange("p (b t) -> p b t", t=15), ps.r[0]))

        def ev_v(m, psv):
            if sample:
                cp('act', V(vb.t[:, m, 0:16 * 23].rearrange("p (b t) -> p b t", t=23)[:, :, 15:23], vb.r[m]),
                   V(psv.ap.rearrange("p (b t) -> p b t", t=8), *psv.res))
                cp('dve', V(vs.t[:, m, 0:128], vs.r[m]), psv)
            else:
                cp('act', V(vb.t[:, m, 15:15 + n], vb.r[m]), psv)
        def winsum(gi, w):
            c0 = 2 * gi
            cur = vb.t[:, c0:c0 + 2, :]
            cur_r = [vb.r[c0], vb.r[c0 + 1]]
            sh = 1
            pi = 0
            while sh < w:
                dst = pw[pi % 2]
                pi += 1
                o_ = vview(dst.t[:, :, :], 2 * sh - 1, L); a_ = vview(cur, 2 * sh - 1, L); b_ = vview(cur, sh - 1, L - sh)
                P.add('dve', (lambda o_, a_, b_: lambda e: e.tensor_tensor(o_, a_, b_, ALU.add))(o_, a_, b_), cur_r, [dst.r[0]])
                cur = dst.t[:, :, :]
                cur_r = [dst.r[0]]
                sh *= 2
            for k in range(2):
                sv = vview(cur, 15, L)[:, k]
                vv = vview(vb.t[:, c0:c0 + 2, :], 15, L)[:, k]
                if sample:
                    po_ = pooled.t[:, c0 + k, 0:n].rearrange("p (b t) -> p b t", t=8)
                else:
                    po_ = pooled.t[:, c0 + k, 0:n]
                P.add('dve', (lambda sv, vv, po_, w: lambda e: e.scalar_tensor_tensor(po_, sv, 1.0 / w, vv, ALU.mult, ALU.subtract))(sv, vv, po_, w),
                      cur_r + [vb.r[c0 + k]], [pooled.r[c0 + k]])
                if first and not sample:
                    t16 = tmp()
                    P.add('dve', (lambda k, gi, cur, t16: lambda e: e.tensor_tensor(t16.t[:, 0:16], cur[:, k, 15:31], ic16.t[:, gi * 16:(gi + 1) * 16], ALU.mult))(k, gi, cur, t16),
                          cur_r + [ic16.r[0]], [t16.r[0]])
                    P.add('dve', (lambda k, c0, t16: lambda e: e.tensor_tensor(pooled.t[:, c0 + k, 0:16], t16.t[:, 0:16], vb.t[:, c0 + k, 15:31], ALU.subtract))(k, c0, t16),
                          [t16.r[0], vb.r[c0 + k]], [pooled.r[c0 + k]])
        for ct_ in range(2):
            linear(w_in, 8, 6176 + ct_ * 512, 512, u, n, (lambda m, psv, off=ct_ * 4: ev_v(m + off, psv)))
            for gi_ in (2 * ct_, 2 * ct_ + 1):
                winsum(gi_, (2, 4, 8, 16)[gi_])
        if sample:
            dma('sp', V(o_pool_s[:, 0:7, :]), V(pool0[:, 8:15, :]), is_out=True)
            st = xst[xst_rr[0] % 2]; xst_rr[0] += 1
            for k0 in (0, 4):
                ps = psum()
                for k in range(4):
                    tr(V(ps.t[:, k * 128:(k + 1) * 128], ps.r[0]), V(vs.t[:, k0 + k, 0:128], vs.r[k0 + k]), Vb(ident_f))
                cp('act' if k0 == 0 else 'dve', V(st.t[:, k0 * 128:(k0 + 4) * 128], st.r[0]), V(ps.t[:, :], ps.r[0]))
            for b in range(16):
                dma('sp', V(o_pool_s[b, 7:15, :]), V(st.t[8 * b:8 * b + 8, :], st.r[0]), is_out=True)
        for gi in range(4):
            wg_ = load_w([(w_pg[gi], 2, 0)])
            for dc in range(2):
                ps = psum()
                for cc in range(2):
                    mm(V(ps.t[:, 0:n], ps.r[0]), V(wg_.t[:, cc, dc * 128:(dc + 1) * 128], wg_.r[0]),
                       V(pooled.t[:, 2 * gi + cc, 0:n], pooled.r[2 * gi + cc]), start=(cc == 0), stop=(cc == 1))
                ts('dve', V(pooled2.t[:, 2 * gi + dc, 0:n], pooled2.r[2 * gi + dc]), V(ps.t[:, 0:n], ps.r[0]), vcol("pscale", 2 * gi + dc), None, ALU.mult)

        if not sample:
            for k in range(8):
                cp('pool', V(ptail.t[:, k, :], ptail.r[k]), V(vb.t[:, k, n:n + 15], vb.r[k]))
            if last:
                for k0 in (0, 4):
                    ps = psum()
                    for k in range(4):
                        tr(V(ps.t[0:15, k * 128:(k + 1) * 128], ps.r[0]), V(ptail.t[:, k0 + k, :], ptail.r[k0 + k]), Vb(ident_f))
                    st = tmp()
                    cp('act', V(st.t[0:15, 0:512], st.r[0]), V(ps.t[0:15, 0:512], ps.r[0]))
                    dma('sp', V(o_pool_p[:, k0 * 128:(k0 + 4) * 128]), V(st.t[0:15, 0:512], st.r[0]), is_out=True)
        if stop_ <= 2:
            return
        def make_group(g):
            idx = [2 * g, 2 * g + 1, 16 + g, 24 + g]
            hs = slice(4 * g, 4 * g + 4)
            ng = negs if sample else negp
            xc = XC[g % 2]
            sz = SZ[g % 2]

            def front():
                if not sample:

                    sA = load_w([(w_in[:, 2048 + 256 * g:2048 + 256 * (g + 1)], 8, 0), (w_in[:, 256 * g:256 * (g + 1)], 8, 256)])
                    sB = load_w([(w_in[:, 4096 + 128 * g:4096 + 128 * (g + 1)], 8, 0), (w_in[:, 5120 + 128 * g:5120 + 128 * (g + 1)], 8, 128)])
                    dg = DG[g % 2]
                    for j in range(4):
                        if first:
                            memset('pool', V(pre16.t[:, j, 0:3], pre16.r[j]), 0.0)
                        else:
                            cp('pool', V(pre16.t[:, j, 0:3], pre16.r[j]), V(ctail.t[:, idx[j], :], ctail.r[idx[j]]))
                    for tap in range(4):
                        for j in range(4):
                            wcol = V(vec_t.t[:, VC["conv_w"] + tap * 32 + idx[j]:VC["conv_w"] + tap * 32 + idx[j] + 1], vec_t.r[0])
                            ts('pool', V(dg.t[:, tap * 4 + j, :], dg.r[0]), Vb(ident_b), wcol, 1.0, ALU.mult, ALU.mult)
                    for j in range(6):
                        slot, col = (sA, j * 128) if j < 2 else ((sB, (j - 2) * 128) if j < 4 else (sA, 256 + (j - 4) * 128))
                        ps = psum()
                        for k in range(8):
                            mm(V(ps.t[:, 0:n], ps.r[0]), V(slot.t[:, k, col:col + 128], slot.r[0]), V(u.t[:, k, 0:n], u.r[k]),
                               start=(k == 0), stop=(k == 7))
                        if j < 4:
                            cp('act', V(pre16.t[:, j, 3:3 + n], pre16.r[j]), V(ps.t[:, 0:n], ps.r[0]))
                            cp('dve', V(ctail.t[:, idx[j], :], ctail.r[idx[j]]), V(ps.t[:, n - 3:n], ps.r[0]))
                        else:
                            act(V(sz.t[:, j - 4, 0:n], sz.r[j - 4]), V(ps.t[:, 0:n], ps.r[0]), AF.Silu)
                    for j in range(4):
                        ps = psum()
                        for tap in range(4):
                            mm(V(ps.t[:, 0:n], ps.r[0]), V(dg.t[:, tap * 4 + j, :], dg.r[0]), V(pre16.t[:, j, tap:tap + n], pre16.r[j]),
                               start=(tap == 0), stop=(tap == 3))
                        act(V(xc.t[:, j, 0:n], xc.r[j]), V(ps.t[:, 0:n], ps.r[0]), AF.Silu, bias=vcol("conv_b", idx[j]))
                    return
                sA = load_w([(w_in[:, 2048 + 256 * g:2048 + 256 * (g + 1)], 8, 0), (w_in[:, 256 * g:256 * (g + 1)], 8, 256)])
                sB = load_w([(w_in[:, 4096 + 128 * g:4096 + 128 * (g + 1)], 8, 0), (w_in[:, 5120 + 128 * g:5120 + 128 * (g + 1)], 8, 128)])

                def pview(j, lo, hi):
                    if sample:
                        return pre.t[:, j, 0:176].rearrange("p (b t) -> p b t", t=11)[:, :, lo:hi]
                    return pre.t[:, j, lo:hi]
                if sample:
                    for j in range(4):
                        dma('sp', V(cst0.t[0:48, j, :], cst0.r[0]), V(conv0[:, idx[j] * 128:(idx[j] + 1) * 128]))
                    ps = psum()
                    for j in range(4):
                        tr(V(ps.t[:, j * 48:(j + 1) * 48], ps.r[0]), V(cst0.t[0:48, j, :], cst0.r[0]), V(ident_f.t[0:48, 0:48], ident_f.r[0]))
                    for j in range(4):
                        cp('dve', V(pview(j, 0, 3), pre.r[j]), V(ps.t[:, j * 48:(j + 1) * 48].rearrange("p (b t) -> p b t", t=3), ps.r[0]))
                else:
                    for j in range(4):
                        if first:
                            memset('pool', V(pview(j, 0, 3), pre.r[j]), 0.0)
                        else:
                            cp('pool', V(pview(j, 0, 3), pre.r[j]), V(ctail.t[:, idx[j], :], ctail.r[idx[j]]))
                for j in range(6):
                    slot, col = (sA, j * 128) if j < 2 else ((sB, (j - 2) * 128) if j < 4 else (sA, 256 + (j - 4) * 128))
                    ps = psum()
                    for k in range(8):
                        mm(V(ps.t[:, 0:n], ps.r[0]), V(slot.t[:, k, col:col + 128], slot.r[0]), V(u.t[:, k, 0:n], u.r[k]),
                           start=(k == 0), stop=(k == 7))
                    if j < 4:
                        if sample:
                            cp('act', V(pview(j, 3, 11), pre.r[j]), V(ps.t[:, 0:n].rearrange("p (b t) -> p b t", t=8), ps.r[0]))
                            tq = tmp()
                            cp('dve', V(tq.t[:, 0:48].rearrange("p (b t) -> p b t", t=3), tq.r[0]),
                               V(ps.t[:, 0:128].rearrange("p (b t) -> p b t", t=8)[:, :, 5:8], ps.r[0]))
                            ps2 = psum()
                            tr(V(ps2.t[0:48, 0:128], ps2.r[0]), V(tq.t[:, 0:48], tq.r[0]), Vb(ident_f))
                            cp('act', V(cso.t[:, j, :], cso.r[0]), V(ps2.t[0:48, 0:128], ps2.r[0]))
                            dma('sp', V(o_conv_s.rearrange("b t c -> (b t) c")[:, idx[j] * 128:(idx[j] + 1) * 128]), V(cso.t[:, j, :], cso.r[0]), is_out=True)
                        else:
                            cp('act', V(pview(j, 3, 3 + n), pre.r[j]), V(ps.t[:, 0:n], ps.r[0]))
                    else:
                        act(V(sz.t[:, j - 4, 0:n], sz.r[j - 4]), V(ps.t[:, 0:n], ps.r[0]), AF.Silu)
                if not sample:
                    for j in range(4):
                        cp('pool', V(ctail.t[:, idx[j], :], ctail.r[idx[j]]), V(pre.t[:, j, n:n + 3], pre.r[j]))
                for tap in range(4):
                    for j in range(4):
                        if sample:
                            src = V(pview(j, tap, tap + 8), pre.r[j])
                            dstv = V(cv.t[:, j, 0:n].rearrange("p (b t) -> p b t", t=8), cv.r[j])
                        else:
                            src = V(pview(j, tap, tap + n), pre.r[j])
                            dstv = V(cv.t[:, j, 0:n], cv.r[j])
                        wcol = V(vec_t.t[:, VC["conv_w"] + tap * 32 + idx[j]:VC["conv_w"] + tap * 32 + idx[j] + 1], vec_t.r[0])
                        if tap == 0:
                            ts('dve', dstv, src, wcol, vcol("conv_b", idx[j]), ALU.mult, ALU.add)
                        else:
                            stt(dstv, src, wcol, dstv, ALU.mult, ALU.add)
                for j in range(4):
                    act(V(xc.t[:, j, 0:n], xc.r[j]), V(cv.t[:, j, 0:n], cv.r[j]), AF.Silu)

                if sample:
                    if g == 0:
                        load_state(0)
                        load_state(1)
                    for b in range(16):
                        s0 = S0h[(2 * g + b // 8) % 3]
                        ps = psum()
                        for j in range(2):
                            tr(V(ps.t[:, j * 128:(j + 1) * 128], ps.r[0]), V(s0.t[:, b % 8, j, :], s0.r[0]), Vb(ident_f))
                        cp('act' if b % 2 else 'dve', V(STs.t[:, b, :], STs.r[0]), V(ps.t[:, 0:256], ps.r[0]))
                    memset('pool', V(Cm.t[:, :, :], Cm.r[0]), 0.0)
                    for b in range(16):
                        cp('pool', V(Cm.t[:, b, 8 * b:8 * b + 8], Cm.r[0]), V(xc.t[:, 3, 8 * b:8 * b + 8], xc.r[3]))


            def p1():
                for c0_ in range(0, nch, 2):
                    nc2 = min(2, nch - c0_)
                    ps = psum()
                    for cc in range(nc2):
                        cs = slice((c0_ + cc) * 128, (c0_ + cc + 1) * 128)
                        for j in range(2):
                            mm(V(ps.t[:, cc * 256 + j * 128:cc * 256 + (j + 1) * 128], ps.r[0]), V(xc.t[:, j, cs], xc.r[j]), Vb(ident_b))
                    cp('act', V(ckb["x_tm"].t[:, c0_:c0_ + nc2, :], ckb["x_tm"].r[0]),
                       V(ps.t[:, 0:nc2 * 256].rearrange("p (c d) -> p c d", c=nc2), ps.r[0]))
                    P.add('dve', (lambda ps, c0_, nc2, hs: lambda e: e.tensor_tensor(
                        ckb["xw"].t[:, c0_:c0_ + nc2, :].rearrange("p c (h d) -> p c h d", h=4),
                        ps.t[:, 0:nc2 * 256].rearrange("p (c h d) -> p c h d", c=nc2, h=4),
                        tm32.t[:, c0_:c0_ + nc2, 2, hs].unsqueeze(3).to_broadcast([128, nc2, 4, 64]), ALU.mult))(ps, c0_, nc2, hs),
                        [ps.r[0], tm32.r[0]], [ckb["xw"].r[0]])
                ps = psum()
                for c in range(nch):
                    cs = slice(c * 128, (c + 1) * 128)
                    mm(V(ps.t[:, cs], ps.r[0]), V(xc.t[:, 2, cs], xc.r[2]), Vb(ident_b))
                cp('act', V(ckb["B_tm"].t[:, 0:nch, :], ckb["B_tm"].r[0]), V(ps.t[:, 0:n].rearrange("p (c d) -> p c d", c=nch), ps.r[0]))
                ps = psum()
                for c in range(nch):
                    cs = slice(c * 128, (c + 1) * 128)
                    mm(V(ps.t[:, cs], ps.r[0]), V(xc.t[:, 2, cs], xc.r[2]), V(xc.t[:, 3, cs], xc.r[3]))
                P.add('dve', (lambda ps, m01: lambda e: e.tensor_tensor(
                    ckb["CBm"].t[:, 0:nch, :], ps.t[:, 0:n].rearrange("p (c d) -> p c d", c=nch),
                    m01.t[:, :].unsqueeze(1).to_broadcast([128, nch, 128]), ALU.mult))(ps, m01),
                    [ps.r[0], m01.r[0]], [ckb["CBm"].r[0]])
                for r_ in range(4):
                    hh = 4 * g + r_
                    ps = psum()
                    mm(V(ps.t[:, 0:n], ps.r[0]), Vb(ident_b), V(ng.t[:, 0:n], ng.r[0]), start=True, stop=False)
                    lhs = ident_f.t[0:32, hh:hh + 1].to_broadcast([32, 128])
                    P.add('pe', (lambda ps, lhs: lambda e: e.matmul(ps.t[:, 0:n], lhs, dtb["acs"].t[:, 0:n], start=False, stop=True))(ps, lhs),
                          [ident_f.r[0], dtb["acs"].r[0]], [ps.r[0]])
                    for c in range(nch):
                        cs = slice(c * 128, (c + 1) * 128)
                        act(V(ckb["W"].t[:, r_, c, :], ckb["W"].r[r_]), V(ps.t[:, cs], ps.r[0]), AF.Exp,
                            bias=V(tm32.t[:, c, 0, hh:hh + 1], tm32.r[0]))
                    tt('dve', V(ckb["W"].t[:, r_, 0:nch, :], ckb["W"].r[r_]), V(ckb["W"].t[:, r_, 0:nch, :], ckb["W"].r[r_]),
                       V(ckb["CBm"].t[:, 0:nch, :], ckb["CBm"].r[0]), ALU.mult)

            def p23():
                for c in range(nch):
                    cs = slice(c * 128, (c + 1) * 128)
                    psd = psum()
                    for r_ in range(4):
                        mm(V(psd.t[:, r_ * 64:(r_ + 1) * 64], psd.r[0]), V(ckb["W"].t[:, r_, c, :], ckb["W"].r[r_]),
                           V(ckb["x_tm"].t[:, c, r_ * 64:(r_ + 1) * 64], ckb["x_tm"].r[0]))
                    pso = psum()
                    if sample:
                        for b in range(16):
                            mm(V(pso.t[:, 0:256], pso.r[0]), V(Cm.t[:, b, :], Cm.r[0]), V(STs.t[:, b, :], STs.r[0]), start=(b == 0), stop=(b == 15))
                    else:
                        mm(V(pso.t[:, 0:256], pso.r[0]), V(xc.t[:, 3, cs], xc.r[3]), V(ST16.t[:, g, :], ST16.r[g]))
                    yo_ = ckb["yo"][c % 2]
                    P.add('dve', (lambda yo_, pso, c, hs: lambda e: e.tensor_tensor(
                        yo_.t[:, :].rearrange("p (h d) -> p h d", h=4), pso.t[:, 0:256].rearrange("p (h d) -> p h d", h=4),
                        tm32.t[:, c, 1, hs].unsqueeze(2).to_broadcast([128, 4, 64]), ALU.mult))(yo_, pso, c, hs),
                        [pso.r[0], tm32.r[0]], [yo_.r[0]])
                    tt('dve', V(ckb["y1"].t[:, c, :], ckb["y1"].r[0]), V(psd.t[:, 0:256], psd.r[0]), Vb(yo_), ALU.add)
                    if sample:
                        P.add('dve', (lambda c: lambda e: e.tensor_tensor(
                            BmA.t[:, :, :], ckb["B_tm"].t[:, c, :].unsqueeze(1).to_broadcast([128, 16, 128]),
                            rowsel.t[:, 0:16].unsqueeze(2).to_broadcast([128, 16, 128]), ALU.mult))(c),
                            [ckb["B_tm"].r[0], rowsel.r[0]], [BmA.r[0]])
                        for hb in range(2):
                            q = 2 * g + hb
                            s0 = S0h[q % 3]
                            if q + 2 < 16:
                                load_state(q + 2)
                            P.add('dve', (lambda g, s0, hb: lambda e: e.tensor_tensor(
                                s0.t[:, :, :, :], s0.t[:, :, :, :],
                                cdpm.t[:, 2 * g:2 * g + 2, hb * 8:(hb + 1) * 8].rearrange("p j b -> p b j").unsqueeze(3).to_broadcast([128, 8, 2, 128]),
                                ALU.mult))(g, s0, hb), [s0.r[0], cdpm.r[0]], [s0.r[0]])
                            for b0_ in range(0, 8, 2):
                                ps = psum()
                                for bb in range(2):
                                    for j in range(2):
                                        mm(V(ps.t[:, (bb * 2 + j) * 128:(bb * 2 + j + 1) * 128], ps.r[0]),
                                           V(ckb["xw"].t[:, c, j * 128:(j + 1) * 128], ckb["xw"].r[0]),
                                           V(BmA.t[:, hb * 8 + b0_ + bb, :], BmA.r[0]))
                                P.add('dve', (lambda s0, ps, b0_: lambda e: e.tensor_tensor(
                                    s0.t[:, b0_:b0_ + 2, :, :], s0.t[:, b0_:b0_ + 2, :, :],
                                    ps.t[:, 0:512].rearrange("p (b j n) -> p b j n", b=2, j=2), ALU.add))(s0, ps, b0_),
                                    [s0.r[0], ps.r[0]], [s0.r[0]])
                            for j in range(2):
                                dma('sp', V(o_ssm_s[hb * 8:(hb + 1) * 8, 4 * g + 2 * j:4 * g + 2 * j + 2].rearrange("b hh p n -> (hh p) b n")),
                                    V(s0.t[:, :, j, :], s0.r[0]), is_out=True)
                    else:
                        ps = psum()
                        for j in range(2):
                            mm(V(ps.t[:, j * 128:(j + 1) * 128], ps.r[0]), V(ckb["xw"].t[:, c, j * 128:(j + 1) * 128], ckb["xw"].r[0]),
                               V(ckb["B_tm"].t[:, c, :], ckb["B_tm"].r[0]))
                        for j in range(2):
                            stt(V(S.t[:, 2 * g + j, :], S.r[2 * g + j]), V(S.t[:, 2 * g + j, :], S.r[2 * g + j]),
                                V(cdpm.t[:, 2 * g + j, c:c + 1], cdpm.r[0]), V(ps.t[:, j * 128:(j + 1) * 128], ps.r[0]), ALU.mult, ALU.add)
                        if not (last and c == nch - 1):
                            ps = psum()
                            for j in range(2):
                                tr(V(ps.t[:, j * 128:(j + 1) * 128], ps.r[0]), V(S.t[:, 2 * g + j, :], S.r[2 * g + j]), Vb(ident_f))
                            cp('act', V(ST16.t[:, g, :], ST16.r[g]), V(ps.t[:, 0:256], ps.r[0]))
                for j in range(2):
                    ps = psum()
                    for c in range(nch):
                        tr(V(ps.t[:, c * 128:(c + 1) * 128], ps.r[0]), V(ckb["y1"].t[:, c, j * 128:(j + 1) * 128], ckb["y1"].r[0]), Vb(ident_f))
                    stt(V(y2.t[:, j, 0:n], y2.r[j]), V(xc.t[:, j, 0:n], xc.r[j]), vcol("dskip", 2 * g + j), V(ps.t[:, 0:n], ps.r[0]),
                        ALU.mult, ALU.add)
                    tt('dve', V(y2.t[:, j, 0:n], y2.r[j]), V(y2.t[:, j, 0:n], y2.r[j]), V(sz.t[:, j, 0:n], sz.r[j]), ALU.mult)
                ya = Buf(y_all.t[:, 2 * g:2 * g + 2, :], 0)
                ya.r = y_all.r[2 * g:2 * g + 2]
                rmsnorm(y2, 2, 256, "g_ssd", 2 * g, ya, n, sq=sq2)


            return front, p1, p23

        groups = [make_group(g) for g in range(8)]
        if sample:
            for (f_, p1_, p23_) in groups:
                f_(); p1_(); p23_()
        else:
            groups[0][0]()
            for g in range(8):
                groups[g][1]()
                if g + 1 < 8:
                    groups[g + 1][0]()
                groups[g][2]()

        if (not sample) and last:
            dma('sp', V(o_ssm_p.rearrange("(j hh) p n -> (hh p) j n", hh=2)), V(S.t[:, :, :], *S.r), is_out=True)
            for k0 in range(0, 32, 4):
                ps = psum()
                for k in range(4):
                    tr(V(ps.t[0:3, k * 128:(k + 1) * 128], ps.r[0]), V(ctail.t[:, k0 + k, :], ctail.r[k0 + k]), Vb(ident_f))
                st = tmp()
                cp('dve', V(st.t[0:3, 0:512], st.r[0]), V(ps.t[0:3, 0:512], ps.r[0]))
                dma('sp', V(o_conv_p[:, k0 * 128:(k0 + 4) * 128]), V(st.t[0:3, 0:512], st.r[0]), is_out=True)

        if stop_ <= 4 or stop_ == 29:
            return
        for ct in range(2):
            wso = [load_w([(w_so[k0 * 128:(k0 + 8) * 128, ct * 512:(ct + 1) * 512], 8, 0)]) for k0 in (0, 8)]
            wg0 = load_w([(w_in[:, 7200 + ct * 512:7200 + (ct + 1) * 512], 8, 0)])
            for m in range(4):
                mi = ct * 4 + m
                pa = psum()
                for k in range(16):
                    mm(V(pa.t[:, 0:n], pa.r[0]), V(wso[k // 8].t[:, k % 8, m * 128:(m + 1) * 128], wso[k // 8].r[0]),
                       V(y_all.t[:, k, 0:n], y_all.r[k]), start=(k == 0), stop=(k == 15))
                pg = psum()
                for k in range(8):
                    mm(V(pg.t[:, 0:n], pg.r[0]), V(wg0.t[:, k, m * 128:(m + 1) * 128], wg0.r[0]), V(u.t[:, k, 0:n], u.r[k]),
                       start=(k == 0), stop=(k == 7))
                s0_ = tmp()
                act(V(s0_.t[:, 0:n], s0_.r[0]), V(pg.t[:, 0:n], pg.r[0]), AF.Sigmoid)
                tt('dve', V(gs.t[:, mi, 0:n], gs.r[mi]), V(pa.t[:, 0:n], pa.r[0]), V(s0_.t[:, 0:n], s0_.r[0]), ALU.mult)
            wpo = load_w([(w_po[:, ct * 512:(ct + 1) * 512], 8, 0)])
            wg1 = load_w([(w_in[:, 8224 + ct * 512:8224 + (ct + 1) * 512], 8, 0)])
            for m in range(4):
                mi = ct * 4 + m
                pb = psum()
                for k in range(8):
                    mm(V(pb.t[:, 0:n], pb.r[0]), V(wpo.t[:, k, m * 128:(m + 1) * 128], wpo.r[0]), V(pooled2.t[:, k, 0:n], pooled2.r[k]),
                       start=(k == 0), stop=(k == 7))
                pg = psum()
                for k in range(8):
                    mm(V(pg.t[:, 0:n], pg.r[0]), V(wg1.t[:, k, m * 128:(m + 1) * 128], wg1.r[0]), V(u.t[:, k, 0:n], u.r[k]),
                       start=(k == 0), stop=(k == 7))
                s1_ = tmp()
                act(V(s1_.t[:, 0:n], s1_.r[0]), V(pg.t[:, 0:n], pg.r[0]), AF.Sigmoid)
                t1_ = tmp()
                tt('dve', V(t1_.t[:, 0:n], t1_.r[0]), V(pb.t[:, 0:n], pb.r[0]), V(s1_.t[:, 0:n], s1_.r[0]), ALU.mult)
                tt('dve', V(merged.t[:, mi, 0:n], merged.r[mi]), V(t1_.t[:, 0:n], t1_.r[0]), V(gs.t[:, mi, 0:n], gs.r[mi]), ALU.add)

        def ev_o(m, psv):
            tt('dve', V(h.t[:, m, 0:n], h.r[m]), psv, V(h.t[:, m, 0:n], h.r[m]), ALU.add)
        linear(w_o, 8, 0, 1024, merged, n, ev_o)

    def ple(p_rows, n):
        rmsnorm(h, 8, 1024, "g_ple", 0, xn, n)
        load_tokens(p_rows, p_fm, n, 2, pst, True)
        for ct in range(2):
            wg = load_w([(w_plg[:, ct * 512:(ct + 1) * 512], 8, 0)])
            wp = load_w([(w_ple[:, ct * 512:(ct + 1) * 512], 2, 0)])
            for m in range(4):
                mi = ct * 4 + m
                pg = psum()
                for k in range(8):
                    mm(V(pg.t[:, 0:n], pg.r[0]), V(wg.t[:, k, m * 128:(m + 1) * 128], wg.r[0]), V(xn.t[:, k, 0:n], xn.r[k]),
                       start=(k == 0), stop=(k == 7))
                pe_ = psum()
                for k in range(2):
                    mm(V(pe_.t[:, 0:n], pe_.r[0]), V(wp.t[:, k, m * 128:(m + 1) * 128], wp.r[0]), V(p_fm.t[:, k, 0:n], p_fm.r[k]),
                       start=(k == 0), stop=(k == 1))
                s_ = tmp()
                act(V(s_.t[:, 0:n], s_.r[0]), V(pg.t[:, 0:n], pg.r[0]), AF.Sigmoid)
                t_ = tmp()
                tt('dve', V(t_.t[:, 0:n], t_.r[0]), V(pe_.t[:, 0:n], pe_.r[0]), V(s_.t[:, 0:n], s_.r[0]), ALU.mult)
                tt('dve', V(h.t[:, mi, 0:n], h.r[mi]), V(h.t[:, mi, 0:n], h.r[mi]), V(t_.t[:, 0:n], t_.r[0]), ALU.add)

    for k in range(16):
        memset('pool', V(S.t[:, k, :], S.r[k]), 0.0)
    for g in range(8):
        memset('pool', V(ST16.t[:, g, :], ST16.r[g]), 0.0)

    blk_ctr = [0]

    npre_x = [0]

    def block(x_rows, p_rows, y_rows, n, sample, first, last, next_x=None):
        cur_blk[0] = blk_ctr[0]
        blk_ctr[0] += 1
        w_idx[0] = 0
        load_tokens(x_rows, h, n, 8, xst, False, npre=npre_x[0])
        npre_x[0] = prefetch_x(next_x) if next_x is not None else 0
        ffn(w_gu1, w_d1, "g_ffn1", n)
        mixer(n, sample, first, last)
        ffn(w_gu2, w_d2, "g_ffn2", n)
        ple(p_rows, n)
        rmsnorm(h, 8, 1024, "g_final", 0, yfin, n)
        store_tokens(yfin, y_rows, n)

    nblk = NBLK if PROMPT_BLOCKS is None else PROMPT_BLOCKS
    for blk in range(nblk):
        rows = slice(blk * T, (blk + 1) * T)
        if blk + 1 < nblk:
            nxt = xp[(blk + 1) * T:(blk + 2) * T, :]
        else:
            nxt = xs if DO_SAMPLE else None
        block(xp[rows, :], pp[rows, :], yp[rows, :], T, False, blk == 0, blk == NBLK - 1, next_x=nxt)
    if DO_SAMPLE:
        block(xs, psm, ys, 128, True, False, False)

    sems = {e: es.enter_context(nc.semaphore("s_" + e)) for e in ('pe', 'act', 'dve', 'pool')}
    dsems = {q: [es.enter_context(nc.semaphore("d_%s%d" % (q, j))) for j in range(NS_DMA)] for q in ('sp', 'pool', 'act')}
    P.emit(nc, sems, dsems)
    es.close()
    return nc


_NC = None


def _pack_vec(i):
    def col(v, n):
        return np.asarray(v, np.float32).reshape(n, 128).T
    cols = [col(i['norm_ffn1'][0], 8), col(i['norm_mix'][0], 8), col(i['norm_ffn2'][0], 8), col(i['norm_ple'][0], 8),
            col(i['norm_final'], 8), col(i['pool_scale'][0], 8), col(i['norm_ssd'][0], 16), col(i['conv_b'][0], 32)]
    cw = np.asarray(i['conv_w'][0], np.float32)
    for k in range(4):
        cols.append(col(cw[k], 32))
    ds = np.asarray(i['d_skip'][0], np.float32)
    dcol = np.repeat(ds, 64).reshape(16, 128).T
    cols.append(dcol)
    return np.ascontiguousarray(np.concatenate(cols, axis=1))


def kernel(**inputs):
    global _NC
    i = {k: np.asarray(v) for k, v in inputs.items()}
    if _NC is None:
        _NC = build()
    nc = _NC
    consts = _consts()
    vec = _pack_vec(i)
    s32 = np.ascontiguousarray(np.stack([i['dt_bias'][0], i['a_log'][0], i['d_skip'][0]], axis=1).astype(np.float32))
    shared = dict(
        w_gu1=i['w_ffn1_gu'][0], w_d1=i['w_ffn1_down'][0], w_gu2=i['w_ffn2_gu'][0], w_d2=i['w_ffn2_down'][0],
        w_in=i['w_in'][0], w_so=i['w_ssd_out'][0], w_pg=i['w_pool_group'][0], w_po=i['w_pool_out'][0], w_o=i['w_o'][0],
        w_plg=i['w_ple_gate'][0], w_ple=i['w_ple'][0], vec=vec, s32=s32)
    for k, v in consts.items():
        shared["c_" + k] = v
    shared = {k: np.ascontiguousarray(v, dtype=np.float32) for k, v in shared.items()}
    in_maps = []
    for c in range(8):
        m = dict(shared)
        m['xp'] = np.ascontiguousarray(i['x_prompt'][c])
        m['xs'] = np.ascontiguousarray(i['x_sample'][16 * c:16 * c + 16].reshape(128, 1024))
        m['ssm0'] = np.ascontiguousarray(i['state_ssm'][0, 16 * c:16 * c + 16])
        m['conv0'] = np.ascontiguousarray(i['state_conv'][0, 16 * c:16 * c + 16].reshape(48, 4096))
        m['pool0'] = np.ascontiguousarray(i['state_pool'][0, 16 * c:16 * c + 16])
        m['pp'] = np.ascontiguousarray(i['p_prompt'][0, c])
        m['psm'] = np.ascontiguousarray(i['p_sample'][0, 16 * c:16 * c + 16].reshape(128, 256))
        in_maps.append(m)
    res = run_bass_kernel_spmd(nc, in_maps, core_ids=list(range(8)))
    rs_ = res.results
    y_prompt = np.stack([r['yp'] for r in rs_], 0)
    y_sample = np.concatenate([r['ys'] for r in rs_], 0).reshape(128, 8, 1024)
    ssm_p = np.stack([r['o_ssm_p'] for r in rs_], 0)[None]
    conv_p = np.stack([r['o_conv_p'] for r in rs_], 0)[None]
    pool_p = np.stack([r['o_pool_p'] for r in rs_], 0)[None]
    ssm_s = np.concatenate([r['o_ssm_s'] for r in rs_], 0)[None]
    conv_s = np.concatenate([r['o_conv_s'] for r in rs_], 0)[None]
    pool_s = np.concatenate([r['o_pool_s'] for r in rs_], 0)[None]
    return (y_prompt, y_sample, ssm_p, conv_p, pool_p, ssm_s, conv_s, pool_s)
```
